# Optimizing a Trainium2 kernel written in Bass

```python
import math
import jax, jax.numpy as jnp
from jax import lax
import numpy as np

D_MODEL = 1024
BATCH = 8
SEQ = 8192
DEPTH = 1
DEC_BATCH = 8
DEC_SEQ = 2048
PAST_LEN = 128

D_SSM = D_MODEL // 2
SSM_GROUP = 16
N_SSM_GROUPS = D_SSM // SSM_GROUP
SSM_STATE = 64
D_HYENA = D_MODEL - D_SSM
HYENA_ORDER = 2
HYENA_SHORT = 3
HYENA_BANDS = 8
HYENA_POS_DIM = 1 + 2 * HYENA_BANDS
HYENA_FILTER_HIDDEN = 64
HYENA_TIME_SCALE = 4096.0
HYENA_MAX_PERIOD = 10000.0
N_FILTERS = HYENA_ORDER * 2 * D_HYENA
D_IN = D_SSM + (HYENA_ORDER + 1) * D_HYENA
D_FF = 128 * math.ceil(8 * D_MODEL / 3 / 128)
LN_EPS = 1e-5
RMS_EPS = 1e-6
FILTER_EPS = 1e-6
DEEPNORM_ALPHA = (2.0 * DEPTH) ** 0.25
DEEPNORM_BETA = (8.0 * DEPTH) ** -0.25

kernel_name = "hybrid_s5_hyena_macaron_encoder"

F32 = jnp.float32


def _layer_norm(x, g, b):
    xf = x.astype(F32)
    mu = jnp.mean(xf, axis=-1, keepdims=True)
    xc = xf - mu
    var = jnp.mean(xc * xc, axis=-1, keepdims=True)
    return (xc * lax.rsqrt(var + LN_EPS) * g.astype(F32) + b.astype(F32)).astype(x.dtype)


def _rms_norm(x, g, dtype):
    xf = x.astype(F32)
    ms = jnp.mean(xf * xf, axis=-1, keepdims=True)
    return (xf * lax.rsqrt(ms + RMS_EPS) * g.astype(F32)).astype(dtype)


def _swiglu(x, w_gate, w_up, w_down):
    return (jax.nn.silu(x @ w_gate) * (x @ w_up)) @ w_down


def _ffn_sublayer(x, w_gate, w_up, w_down, ln_g, ln_b):
    return _layer_norm(DEEPNORM_ALPHA * x + 0.5 * _swiglu(x, w_gate, w_up, w_down), ln_g, ln_b)


def _short_conv(x, w, b):
    L = x.shape[1]
    pad = HYENA_SHORT // 2
    xp = jnp.pad(x, ((0, 0), (pad, pad), (0, 0)))
    y = b
    for j in range(HYENA_SHORT):
        y = y + xp[:, j:j + L] * w[j]
    return y


def _complex_linear_combine(e1, e2):
    a1r, a1i, b1r, b1i = e1
    a2r, a2i, b2r, b2i = e2
    ar = a2r * a1r - a2i * a1i
    ai = a2r * a1i + a2i * a1r
    br = a2r * b1r - a2i * b1i + b2r
    bi = a2r * b1i + a2i * b1r + b2i
    return ar, ai, br, bi


def _s5_discretize(lam_re, lam_im, log_step, b_re, b_im):
    step = jnp.exp(log_step)[:, None]
    mag = jnp.exp(lam_re * step)
    ar = mag * jnp.cos(lam_im * step)
    ai = mag * jnp.sin(lam_im * step)
    nr = ar - 1.0
    ni = ai
    den = lam_re * lam_re + lam_im * lam_im
    qr = (nr * lam_re + ni * lam_im) / den
    qi = (ni * lam_re - nr * lam_im) / den
    bbr = qr[..., None] * b_re - qi[..., None] * b_im
    bbi = qr[..., None] * b_im + qi[..., None] * b_re
    return ar, ai, bbr, bbi


def _s5_scan(u, ar, ai, bbr, bbi, c_re, c_im, reverse):
    L = u.shape[0]
    bur = jnp.einsum('lgh,gph->lgp', u, bbr)
    bui = jnp.einsum('lgh,gph->lgp', u, bbi)
    a_r = jnp.broadcast_to(ar, (L,) + ar.shape)
    a_i = jnp.broadcast_to(ai, (L,) + ai.shape)
    _, _, hr, hi = lax.associative_scan(_complex_linear_combine, (a_r, a_i, bur, bui),
                                        reverse=reverse, axis=0)
    return jnp.einsum('lgp,ghp->lgh', hr, c_re) - jnp.einsum('lgp,ghp->lgh', hi, c_im)


def _s5_mixer(u, lam_re, lam_im, log_step, b_re, b_im, c_re, c_im, d):
    nb, L, _ = u.shape
    ug = u.astype(F32).reshape(nb, L, N_SSM_GROUPS, SSM_GROUP)
    lam_re, lam_im, log_step = lam_re.astype(F32), lam_im.astype(F32), log_step.astype(F32)
    b_re, b_im, c_re, c_im = b_re.astype(F32), b_im.astype(F32), c_re.astype(F32), c_im.astype(F32)
    d = d.astype(F32)
    fwd = _s5_discretize(lam_re[0], lam_im[0], log_step[0], b_re[0], b_im[0])
    bwd = _s5_discretize(lam_re[1], lam_im[1], log_step[1], b_re[1], b_im[1])

    def one_sequence(us):
        yf = _s5_scan(us, *fwd, c_re[0], c_im[0], False)
        yb = _s5_scan(us, *bwd, c_re[1], c_im[1], True)
        return yf + yb + d * us

    y = lax.map(one_sequence, ug)
    return y.reshape(nb, L, D_SSM)


def _hyena_filter_spectrum(L, w1, b1, w2, b2, w3, sin_freq, log_decay):
    w1, b1, w2, b2, w3 = (a.astype(F32) for a in (w1, b1, w2, b2, w3))
    sin_freq, log_decay = sin_freq.astype(F32), log_decay.astype(F32)
    t = jnp.arange(L, dtype=F32)
    t_lin = t / HYENA_TIME_SCALE
    omega = jnp.exp(-math.log(HYENA_MAX_PERIOD) * jnp.arange(HYENA_BANDS, dtype=F32) / HYENA_BANDS)
    ang = t[:, None] * omega[None, :]
    feats = jnp.concatenate([t_lin[:, None], jnp.sin(ang), jnp.cos(ang)], axis=-1)
    hdn = jnp.sin(sin_freq[0] * (feats @ w1 + b1))
    hdn = jnp.sin(sin_freq[1] * (hdn @ w2 + b2))
    filt = (hdn @ w3) * jnp.exp(-t_lin[:, None] * jnp.exp(log_decay)[None, :])
    filt = filt.reshape(L, HYENA_ORDER, 2, D_HYENA)
    fwd = filt[:, :, 0]
    bwd = filt[:, :, 1]
    k = jnp.concatenate([fwd, jnp.zeros((1, HYENA_ORDER, D_HYENA), F32), bwd[:0:-1]], axis=0)
    k = k / (jnp.sum(jnp.abs(k), axis=0, keepdims=True) + FILTER_EPS)
    return jnp.fft.rfft(k, axis=0)


def _hyena_mixer(v, gates, k_f, bias):
    L = v.shape[1]
    n = 2 * L
    z = v
    for o in range(HYENA_ORDER):
        zf = jnp.fft.rfft(z, n=n, axis=1)
        y = jnp.fft.irfft(zf * k_f[None, :, o], n=n, axis=1)[:, :L]
        z = gates[o] * (y + bias[o] * z)
    return z


def _mixing_sublayer(h, w_in,
                     ssm_lam_re, ssm_lam_im, ssm_log_step, ssm_b_re, ssm_b_im, ssm_c_re, ssm_c_im,
                     ssm_d, ssm_glu_w, ssm_glu_b, ssm_norm_g,
                     hy_short_w, hy_short_b, hy_filt_w1, hy_filt_b1, hy_filt_w2, hy_filt_b2,
                     hy_filt_w3, hy_sin_freq, hy_log_decay, hy_bias, hy_norm_g, w_out):
    proj = h @ w_in
    u = proj[..., :D_SSM]
    hy = _short_conv(proj[..., D_SSM:], hy_short_w, hy_short_b).astype(F32)
    v = hy[..., :D_HYENA]
    gates = [hy[..., D_HYENA * (o + 1):D_HYENA * (o + 2)] for o in range(HYENA_ORDER)]

    y_ssm = _s5_mixer(u, ssm_lam_re, ssm_lam_im, ssm_log_step, ssm_b_re, ssm_b_im,
                      ssm_c_re, ssm_c_im, ssm_d)
    g = jax.nn.gelu(y_ssm)
    y_ssm = g * jax.nn.sigmoid(g @ ssm_glu_w.astype(F32) + ssm_glu_b.astype(F32))

    k_f = _hyena_filter_spectrum(h.shape[1], hy_filt_w1, hy_filt_b1, hy_filt_w2, hy_filt_b2,
                                 hy_filt_w3, hy_sin_freq, hy_log_decay)
    y_hy = _hyena_mixer(v, gates, k_f, hy_bias.astype(F32))

    mixed = jnp.concatenate([_rms_norm(y_ssm, ssm_norm_g, h.dtype),
                             _rms_norm(y_hy, hy_norm_g, h.dtype)], axis=-1)
    return mixed @ w_out


def _run_trunk(x, ffn1_w_gate, ffn1_w_up, ffn1_w_down, ln1_g, ln1_b, w_in,
               ssm_lam_re, ssm_lam_im, ssm_log_step, ssm_b_re, ssm_b_im, ssm_c_re, ssm_c_im,
               ssm_d, ssm_glu_w, ssm_glu_b, ssm_norm_g,
               hy_short_w, hy_short_b, hy_filt_w1, hy_filt_b1, hy_filt_w2, hy_filt_b2, hy_filt_w3,
               hy_sin_freq, hy_log_decay, hy_bias, hy_norm_g, w_out, ln2_g, ln2_b,
               ffn2_w_gate, ffn2_w_up, ffn2_w_down, ln3_g, ln3_b):
    for l in range(DEPTH):
        x = _ffn_sublayer(x, ffn1_w_gate[l], ffn1_w_up[l], ffn1_w_down[l], ln1_g[l], ln1_b[l])
        mix = _mixing_sublayer(x, w_in[l],
                               ssm_lam_re[l], ssm_lam_im[l], ssm_log_step[l], ssm_b_re[l], ssm_b_im[l],
                               ssm_c_re[l], ssm_c_im[l], ssm_d[l], ssm_glu_w[l], ssm_glu_b[l], ssm_norm_g[l],
                               hy_short_w[l], hy_short_b[l], hy_filt_w1[l], hy_filt_b1[l], hy_filt_w2[l],
                               hy_filt_b2[l], hy_filt_w3[l], hy_sin_freq[l], hy_log_decay[l], hy_bias[l],
                               hy_norm_g[l], w_out[l])
        x = _layer_norm(DEEPNORM_ALPHA * x + mix, ln2_g[l], ln2_b[l])
        x = _ffn_sublayer(x, ffn2_w_gate[l], ffn2_w_up[l], ffn2_w_down[l], ln3_g[l], ln3_b[l])
    return x


def setup_inputs(seed: int = 0) -> dict:
    key = jax.random.key(seed)
    ks = iter(jax.random.split(key, 48))
    nrm = lambda shape, scale: scale * jax.random.normal(next(ks), shape, F32)
    G, H, P = N_SSM_GROUPS, SSM_GROUP, SSM_STATE
    Dp = DEPTH

    x_prompt = jax.random.normal(next(ks), (BATCH, SEQ, D_MODEL), F32)
    x_sample = jax.random.normal(next(ks), (DEC_BATCH, DEC_SEQ, D_MODEL), F32)

    ffn1_w_gate = nrm((Dp, D_MODEL, D_FF), D_MODEL ** -0.5)
    ffn1_w_up = nrm((Dp, D_MODEL, D_FF), D_MODEL ** -0.5)
    ffn1_w_down = nrm((Dp, D_FF, D_MODEL), DEEPNORM_BETA * D_FF ** -0.5)
    ln1_g = 1.0 + nrm((Dp, D_MODEL), 0.02)
    ln1_b = nrm((Dp, D_MODEL), 0.02)

    w_in = nrm((Dp, D_MODEL, D_IN), D_MODEL ** -0.5)

    ssm_lam_re = -0.5 + nrm((Dp, 2, G, P), 0.01)
    ssm_lam_im = jnp.pi * jnp.arange(P, dtype=F32) + nrm((Dp, 2, G, P), 0.01)
    ssm_log_step = jax.random.uniform(next(ks), (Dp, 2, G), F32, math.log(1e-3), math.log(1e-1))
    ssm_b_re = nrm((Dp, 2, G, P, H), (2.0 * H) ** -0.5)
    ssm_b_im = nrm((Dp, 2, G, P, H), (2.0 * H) ** -0.5)
    ssm_c_re = nrm((Dp, 2, G, H, P), (2.0 * P) ** -0.5)
    ssm_c_im = nrm((Dp, 2, G, H, P), (2.0 * P) ** -0.5)
    ssm_d = nrm((Dp, G, H), 1.0)
    ssm_glu_w = nrm((Dp, D_SSM, D_SSM), D_SSM ** -0.5)
    ssm_glu_b = nrm((Dp, D_SSM), 0.02)
    ssm_norm_g = 1.0 + nrm((Dp, D_SSM), 0.02)

    hy_short_w = nrm((Dp, HYENA_SHORT, (HYENA_ORDER + 1) * D_HYENA), HYENA_SHORT ** -0.5)
    hy_short_b = nrm((Dp, (HYENA_ORDER + 1) * D_HYENA), 0.02)
    hy_filt_w1 = nrm((Dp, HYENA_POS_DIM, HYENA_FILTER_HIDDEN), HYENA_POS_DIM ** -0.5)
    hy_filt_b1 = nrm((Dp, HYENA_FILTER_HIDDEN), 0.1)
    hy_filt_w2 = nrm((Dp, HYENA_FILTER_HIDDEN, HYENA_FILTER_HIDDEN), HYENA_FILTER_HIDDEN ** -0.5)
    hy_filt_b2 = nrm((Dp, HYENA_FILTER_HIDDEN), 0.1)
    hy_filt_w3 = nrm((Dp, HYENA_FILTER_HIDDEN, N_FILTERS), HYENA_FILTER_HIDDEN ** -0.5)
    hy_sin_freq = 1.0 + nrm((Dp, 2, HYENA_FILTER_HIDDEN), 0.1)
    fast, slow = math.log(abs(math.log(1e-2)) / 0.3), math.log(abs(math.log(1e-2)) / 1.5)
    base_decay = jnp.tile(jnp.linspace(fast, slow, D_HYENA, dtype=F32), HYENA_ORDER * 2)
    hy_log_decay = base_decay + nrm((Dp, N_FILTERS), 0.01)
    hy_bias = nrm((Dp, HYENA_ORDER, D_HYENA), 1.0)
    hy_norm_g = 1.0 + nrm((Dp, D_HYENA), 0.02)

    w_out = nrm((Dp, D_MODEL, D_MODEL), DEEPNORM_BETA * D_MODEL ** -0.5)
    ln2_g = 1.0 + nrm((Dp, D_MODEL), 0.02)
    ln2_b = nrm((Dp, D_MODEL), 0.02)

    ffn2_w_gate = nrm((Dp, D_MODEL, D_FF), D_MODEL ** -0.5)
    ffn2_w_up = nrm((Dp, D_MODEL, D_FF), D_MODEL ** -0.5)
    ffn2_w_down = nrm((Dp, D_FF, D_MODEL), DEEPNORM_BETA * D_FF ** -0.5)
    ln3_g = 1.0 + nrm((Dp, D_MODEL), 0.02)
    ln3_b = nrm((Dp, D_MODEL), 0.02)

    return {
        "x_prompt": x_prompt, "x_sample": x_sample,
        "ffn1_w_gate": ffn1_w_gate, "ffn1_w_up": ffn1_w_up, "ffn1_w_down": ffn1_w_down,
        "ln1_g": ln1_g, "ln1_b": ln1_b, "w_in": w_in,
        "ssm_lam_re": ssm_lam_re, "ssm_lam_im": ssm_lam_im, "ssm_log_step": ssm_log_step,
        "ssm_b_re": ssm_b_re, "ssm_b_im": ssm_b_im, "ssm_c_re": ssm_c_re, "ssm_c_im": ssm_c_im,
        "ssm_d": ssm_d, "ssm_glu_w": ssm_glu_w, "ssm_glu_b": ssm_glu_b, "ssm_norm_g": ssm_norm_g,
        "hy_short_w": hy_short_w, "hy_short_b": hy_short_b,
        "hy_filt_w1": hy_filt_w1, "hy_filt_b1": hy_filt_b1, "hy_filt_w2": hy_filt_w2,
        "hy_filt_b2": hy_filt_b2, "hy_filt_w3": hy_filt_w3, "hy_sin_freq": hy_sin_freq,
        "hy_log_decay": hy_log_decay, "hy_bias": hy_bias, "hy_norm_g": hy_norm_g,
        "w_out": w_out, "ln2_g": ln2_g, "ln2_b": ln2_b,
        "ffn2_w_gate": ffn2_w_gate, "ffn2_w_up": ffn2_w_up, "ffn2_w_down": ffn2_w_down,
        "ln3_g": ln3_g, "ln3_b": ln3_b,
    }


def reference(x_prompt, x_sample, ffn1_w_gate, ffn1_w_up, ffn1_w_down, ln1_g, ln1_b, w_in,
              ssm_lam_re, ssm_lam_im, ssm_log_step, ssm_b_re, ssm_b_im, ssm_c_re, ssm_c_im,
              ssm_d, ssm_glu_w, ssm_glu_b, ssm_norm_g,
              hy_short_w, hy_short_b, hy_filt_w1, hy_filt_b1, hy_filt_w2, hy_filt_b2, hy_filt_w3,
              hy_sin_freq, hy_log_decay, hy_bias, hy_norm_g, w_out, ln2_g, ln2_b,
              ffn2_w_gate, ffn2_w_up, ffn2_w_down, ln3_g, ln3_b):
    y_prompt = _run_trunk(x_prompt, ffn1_w_gate, ffn1_w_up, ffn1_w_down, ln1_g, ln1_b, w_in,
                          ssm_lam_re, ssm_lam_im, ssm_log_step, ssm_b_re, ssm_b_im, ssm_c_re, ssm_c_im,
                          ssm_d, ssm_glu_w, ssm_glu_b, ssm_norm_g,
                          hy_short_w, hy_short_b, hy_filt_w1, hy_filt_b1, hy_filt_w2, hy_filt_b2, hy_filt_w3,
                          hy_sin_freq, hy_log_decay, hy_bias, hy_norm_g, w_out, ln2_g, ln2_b,
                          ffn2_w_gate, ffn2_w_up, ffn2_w_down, ln3_g, ln3_b)
    y_sample = _run_trunk(x_sample, ffn1_w_gate, ffn1_w_up, ffn1_w_down, ln1_g, ln1_b, w_in,
                          ssm_lam_re, ssm_lam_im, ssm_log_step, ssm_b_re, ssm_b_im, ssm_c_re, ssm_c_im,
                          ssm_d, ssm_glu_w, ssm_glu_b, ssm_norm_g,
                          hy_short_w, hy_short_b, hy_filt_w1, hy_filt_b1, hy_filt_w2, hy_filt_b2, hy_filt_w3,
                          hy_sin_freq, hy_log_decay, hy_bias, hy_norm_g, w_out, ln2_g, ln2_b,
                          ffn2_w_gate, ffn2_w_up, ffn2_w_down, ln3_g, ln3_b)
    return (y_prompt, y_sample)
```

```python
import math
from contextlib import ExitStack, contextmanager
import numpy as np
import concourse.bass as bass
import concourse.mybir as mybir
from concourse.bass_utils import run_bass_kernel_spmd

F32 = mybir.dt.float32
R32 = mybir.dt.float32r
BF16 = mybir.dt.bfloat16
ALU = mybir.AluOpType
AF = mybir.ActivationFunctionType
AX = mybir.AxisListType

ENGS = ("pe", "act", "dve", "pool", "sp")
_SKIP = set()
D, DFF, DS, DH, DIN = 1024, 2816, 512, 512, 2048
KC, FC = 8, 22
TT = 512
ALPHA = 2.0 ** 0.25
LN_EPS, RMS_EPS, FILTER_EPS = 1e-5, 1e-6, 1e-6
MAGIC = 12582912.0
TWO_PI = 2.0 * math.pi


class Prog:
    def __init__(self, nc, n_dma_sems=8):
        self.nc = nc
        self.root = ExitStack()
        self.stacks = [self.root]
        self.eng = {"pe": nc.tensor, "act": nc.scalar, "dve": nc.vector,
                    "pool": nc.gpsimd, "sp": nc.sync}
        self.ops = []
        self.res = {}
        self.n_dma_sems = n_dma_sems
        self.dma_rr = {e: 0 for e in ENGS}
        self.dma_last = {}
        self.dma_cnt = {}
        self.last_op = {}
        self.uid = 0

    def sb(self, name, shape, dt=F32):
        self.uid += 1
        return self.stacks[-1].enter_context(self.nc.sbuf_tensor("%s_%d" % (name, self.uid), list(shape), dt))

    def ps(self, name, shape, dt=F32):
        self.uid += 1
        return self.stacks[-1].enter_context(self.nc.psum_tensor("%s_%d" % (name, self.uid), list(shape), dt))

    @contextmanager
    def phase(self):
        st = ExitStack()
        self.stacks.append(st)
        try:
            yield
        finally:
            self.barrier()
            self.emit_pending()
            self.stacks.pop()
            st.close()

    def _deps(self, reads, writes):
        deps = set()
        for k in reads:
            st = self.res.get(k)
            if st is not None and st[0] is not None:
                deps.add(st[0])
        for k in writes:
            st = self.res.get(k)
            if st is not None:
                if st[0] is not None:
                    deps.add(st[0])
                deps.update(st[1])
        return deps

    def _commit(self, oid, reads, writes):
        for k in reads:
            st = self.res.setdefault(k, [None, []])
            st[1].append(oid)
        for k in writes:
            self.res[k] = [oid, []]

    def op(self, eng, name, *args, reads=(), writes=(), **kw):
        deps = self._deps(reads, writes)
        oid = len(self.ops)
        self.ops.append(dict(eng=eng, kind="op", name=name, args=args, kw=kw, deps=deps))
        self._commit(oid, reads, writes)
        self.last_op[eng] = oid
        return oid

    def dma(self, out, in_, reads=(), writes=(), eng="sp", **kw):
        deps = self._deps(reads, writes)
        k = self.dma_rr[eng]
        self.dma_rr[eng] = (k + 1) % self.n_dma_sems
        prev = self.dma_last.get((eng, k))
        if prev is not None:
            deps.add(prev)
        cnt = self.dma_cnt.get((eng, k), 0) + 1
        self.dma_cnt[(eng, k)] = cnt
        oid = len(self.ops)
        self.ops.append(dict(eng=eng, kind="dma", out=out, in_=in_, kw=kw, deps=deps,
                             semk=(eng, k), val=16 * cnt))
        self.dma_last[(eng, k)] = oid
        self._commit(oid, reads, writes)
        return oid

    def barrier(self):
        deps = set(self.last_op.values()) | set(self.dma_last.values())
        for e in ENGS:
            oid = len(self.ops)
            self.ops.append(dict(eng=e, kind="bar", deps=set(deps)))
            self.last_op[e] = oid
        self.res = {}

    def emit_pending(self):
        nc = self.nc
        ops = self.ops
        if not hasattr(self, "esem"):
            self.esem = {e: self.root.enter_context(nc.semaphore("es_" + e)) for e in ENGS}
            self.dsem = {}
            self.cnt = {e: 0 for e in ENGS}
            self.sig = []
            self.waited = {e: {} for e in ENGS}
            self.nwait = 0
            self.emitted = 0
        start = self.emitted
        needed = [False] * len(ops)
        for o in ops[start:]:
            for d in o["deps"]:
                needed[d] = True
        esem, dsem, cnt, sig, waited = self.esem, self.dsem, self.cnt, self.sig, self.waited
        sig.extend([None] * (len(ops) - len(sig)))
        for i in range(start, len(ops)):
            o = ops[i]
            e = o["eng"]
            E = self.eng[e]
            kind = o["kind"]
            w = {}
            for d in o["deps"]:
                if sig[d] is None:
                    assert ops[d]["kind"] == "bar" or d >= start, (i, d, ops[d]["kind"])
                    if ops[d]["kind"] != "bar":
                        raise RuntimeError("dependency on unsignalled op")
                    continue
                s_, v = sig[d]
                if ops[d]["eng"] == e and ops[d]["kind"] == "op" and e == "pe" and kind == "op":
                    continue
                if w.get(id(s_), (None, 0))[1] < v:
                    w[id(s_)] = (s_, v)
            for s_, v in w.values():
                if waited[e].get(id(s_), 0) >= v:
                    continue
                waited[e][id(s_)] = v
                E.wait_ge(s_, v)
                self.nwait += 1
            if kind == "bar":
                sig[i] = None
            elif kind == "dma":
                if o["semk"] not in dsem:
                    dsem[o["semk"]] = self.root.enter_context(nc.semaphore("ds_%s_%d" % o["semk"]))
                s_ = dsem[o["semk"]]
                E.dma_start(out=o["out"], in_=o["in_"], **o["kw"]).then_inc(s_, 16)
                sig[i] = (s_, o["val"])
            else:
                ins = getattr(E, o["name"])(*o["args"], **o["kw"])
                if needed[i]:
                    cnt[e] += 1
                    ins.then_inc(esem[e], 1)
                    sig[i] = (esem[e], cnt[e])
            o["args"] = None; o["kw"] = None; o["out"] = None; o["in_"] = None
        self.emitted = len(ops)

    def emit(self):
        self.barrier()
        self.emit_pending()
        nc = self.nc
        for key, s_ in self.dsem.items():
            v = 16 * self.dma_cnt[key]
            if self.waited["sp"].get(id(s_), 0) < v:
                nc.sync.wait_ge(s_, v)
        self.stats = dict(n_ops=len(self.ops), n_wait=self.nwait, cnt=dict(self.cnt))
        self.root.close()


def fft_dims(L):
    N = 2 * L
    lg = int(round(math.log2(N)))
    n1 = 1 << ((lg + 1) // 2)
    n2 = N // n1
    return N, n1, n2


def fft_consts(L):
    N, N1, N2 = fft_dims(L)
    a = np.arange(N1)[:, None]; fa = np.arange(N1)[None, :]
    b = np.arange(N2)[:, None]; fb = np.arange(N2)[None, :]
    w1 = 2 * np.pi * (a * fa % N1) / N1
    w2 = 2 * np.pi * (b * fb % N2) / N2
    c = {}
    c["F1"] = np.concatenate([np.cos(w1), -np.sin(w1)], 1)
    c["F2re"] = np.cos(w2); c["F2im"] = -np.sin(w2); c["nF2im"] = np.sin(w2)
    c["G2a"] = np.concatenate([np.cos(w2), np.sin(w2)], 1)
    c["G2b"] = np.concatenate([-np.sin(w2), np.cos(w2)], 1)
    h = N1 // 2
    c["G1re"] = (np.cos(w1) / N)[:, :h]
    c["nG1im"] = (-np.sin(w1) / N)[:, :h]
    tw = 2 * np.pi * ((np.arange(N2)[:, None] * np.arange(N1)[None, :]) % N) / N
    c["TWrr"] = np.concatenate([np.cos(tw), np.cos(tw)], 1)
    c["TWis"] = np.concatenate([np.sin(tw), -np.sin(tw)], 1)
    twc = tw.T
    c["TcRR"] = np.concatenate([np.cos(twc), np.cos(twc)], 1)
    c["TcIS"] = np.concatenate([-np.sin(twc), np.sin(twc)], 1)
    return {k: np.ascontiguousarray(v, dtype=np.float32) for k, v in c.items()}


def filter_pos_tables(L):
    N = 2 * L
    m = np.arange(N)
    pos = np.where(m < L, m, N - m).astype(np.float32)
    pos[L] = 0.0
    t_lin = (pos / np.float32(4096.0)).astype(np.float32)
    omega = np.exp(np.float32(-math.log(10000.0)) * np.arange(8, dtype=np.float32) / np.float32(8)).astype(np.float32)
    ang = (pos[:, None] * omega[None, :]).astype(np.float32)
    feats = np.concatenate([t_lin[:, None], np.sin(ang), np.cos(ang)], -1).astype(np.float32)
    return np.ascontiguousarray(feats.T), np.ascontiguousarray(t_lin[None, :])


COLS = {}
_off = 0
for _n, _w in (("ln1g", 8), ("ln1b", 8), ("ln2g", 8), ("ln2b", 8), ("ln3g", 8), ("ln3b", 8),
               ("glub", 4), ("sng", 4), ("hng", 4), ("sw0", 12), ("sw1", 12), ("sw2", 12), ("sb", 12)):
    COLS[_n] = (_off, _w)
    _off += _w
NCOLS = _off


def col_layout(v):
    return np.ascontiguousarray(np.asarray(v, np.float32).reshape(-1, 128).T)


def host_inputs(inp, core, Lp, Ls):
    g = lambda k: np.asarray(inp[k], np.float32)[0]
    m = {}
    m["x"] = np.ascontiguousarray(np.concatenate(
        [np.asarray(inp["x_prompt"], np.float32)[core, :Lp], np.asarray(inp["x_sample"], np.float32)[core, :Ls]], 0))
    for k, src in (("wg1", "ffn1_w_gate"), ("wu1", "ffn1_w_up"), ("wd1", "ffn1_w_down"), ("win", "w_in"),
                   ("wout", "w_out"), ("wg2", "ffn2_w_gate"), ("wu2", "ffn2_w_up"), ("wd2", "ffn2_w_down"),
                   ("gluw", "ssm_glu_w"), ("fw1", "hy_filt_w1"), ("fw2", "hy_filt_w2"), ("fw3", "hy_filt_w3")):
        m[k] = np.ascontiguousarray(g(src))
    cols = np.zeros((128, NCOLS), np.float32)
    def put(name, v):
        o, w = COLS[name]
        cols[:, o:o + w] = col_layout(v)
    put("ln1g", g("ln1_g")); put("ln1b", g("ln1_b")); put("ln2g", g("ln2_g")); put("ln2b", g("ln2_b"))
    put("ln3g", g("ln3_g")); put("ln3b", g("ln3_b")); put("glub", g("ssm_glu_b")); put("sng", g("ssm_norm_g"))
    put("hng", g("hy_norm_g"))
    sw = g("hy_short_w")
    put("sw0", sw[0]); put("sw1", sw[1]); put("sw2", sw[2]); put("sb", g("hy_short_b"))
    m["cols"] = cols
    m["hbias"] = np.ascontiguousarray(np.broadcast_to(g("hy_bias").reshape(1, 1024), (128, 1024)))
    sf = g("hy_sin_freq")
    m["fcols"] = np.ascontiguousarray(np.stack([sf[0], g("hy_filt_b1"), sf[1], g("hy_filt_b2")], 1))
    m["logdec"] = np.ascontiguousarray(g("hy_log_decay").reshape(1, 2048))
    lre, lim, lst = g("ssm_lam_re"), g("ssm_lam_im"), g("ssm_log_step")
    bre, bim, cre, cim = g("ssm_b_re"), g("ssm_b_im"), g("ssm_c_re"), g("ssm_c_im")
    lam_pad = np.zeros((3, 128, 2, 16, 128), np.float32)
    B_pad = np.zeros((2, 128, 2, 16, 128), np.float32)
    C_pad = np.zeros((2, 128, 2, 16, 128), np.float32)
    for G in range(16):
        for g2 in range(2):
            gg = 2 * G + g2
            sl = slice(g2 * 64, g2 * 64 + 64)
            lam_pad[0, :, :, G, sl] = lre[:, gg, :][None]
            lam_pad[1, :, :, G, sl] = lim[:, gg, :][None]
            lam_pad[2, :, :, G, sl] = lst[:, gg][None, :, None]
            q0 = (G % 4) * 32 + g2 * 16
            for d in range(2):
                B_pad[0, q0:q0 + 16, d, G, sl] = bre[d, gg].T
                B_pad[1, q0:q0 + 16, d, G, sl] = bim[d, gg].T
                C_pad[0, sl, d, G, q0:q0 + 16] = cre[d, gg].T
                C_pad[1, sl, d, G, q0:q0 + 16] = cim[d, gg].T
    m["lam_pad"] = lam_pad.reshape(3, 128, 4096)
    m["B_pad"] = B_pad.reshape(2, 128, 4096)
    m["C_pad"] = C_pad.reshape(2, 128, 4096)
    lam_s = np.zeros((3, 128, 2, 16), np.float32)
    for G in range(16):
        for g2 in range(2):
            gg = 2 * G + g2
            sl = slice(g2 * 64, g2 * 64 + 64)
            lam_s[0, sl, :, G] = lre[:, gg, :].T
            lam_s[1, sl, :, G] = lim[:, gg, :].T
            lam_s[2, sl, :, G] = lst[:, gg][None, :]
    m["lam_s"] = lam_s.reshape(3, 128, 32)
    dsk = g("ssm_d").reshape(512)
    D_pad = np.zeros((128, 4, 128), np.float32)
    for ct in range(4):
        D_pad[np.arange(128), ct, np.arange(128)] = dsk[ct * 128:(ct + 1) * 128]
    m["D_pad"] = D_pad.reshape(128, 512)
    m["ident"] = np.eye(128, dtype=np.float32)
    for si, L in enumerate((Lp, Ls)):
        for k, v in fft_consts(L).items():
            m["c%d_%s" % (si, k)] = v
        ft, tl = filter_pos_tables(L)
        m["c%d_feats" % si] = ft
        m["c%d_tl" % si] = tl
    return m


def build(Lp, Ls, shapes, debug=False):
    nc = bass.Bass("TRN2", target_bir_lowering=False)
    P = Prog(nc)
    Ltot = Lp + Ls
    seqs = [(0, Lp), (Lp, Ls)]
    I = {k: nc.dram_tensor(k, list(s), F32, kind="ExternalInput").ap() for k, s in shapes.items()}
    yout = nc.dram_tensor("y", [Ltot, D], F32, kind="ExternalOutput").ap()
    skind = "ExternalOutput" if debug else "Internal"
    def scratch(name, shape):
        return nc.dram_tensor(name, list(shape), F32, kind=skind).ap()
    X1T = scratch("X1T", [D, Ltot])
    US = scratch("US", [DS, Ltot])
    VS = [scratch("VS%d" % i, [DH, Ltot]) for i in range(3)]
    YS = scratch("YS", [DS, Ltot])
    YH = scratch("YH", [DH, Ltot])
    KT = [scratch("KT%d" % si, [2, DH, 2 * L]) for si, (_, L) in enumerate(seqs)]
    BBW = scratch("BBW", [2, 128, 2 * 8 * 2048])
    WSPEC = {"wg1": (D, DFF), "wu1": (D, DFF), "wd1": (DFF, D), "win": (D, DIN), "gluw": (DS, DS), "wout": (D, D),
             "wg2": (D, DFF), "wu2": (D, DFF), "wd2": (DFF, D)}
    WS = {}
    FFNW = ("wg1", "wu1", "wd1", "wg2", "wu2", "wd2", "win", "gluw", "wout")
    for wn, (kin, nout) in WSPEC.items():
        WS[wn] = nc.dram_tensor("WS_" + wn, [nout // 128, 128, kin], BF16 if wn in FFNW else F32, kind="Internal").ap()

    V, S, G_, T = "dve", "act", "pool", "pe"

    def f32(ap):
        return ap.bitcast(F32)

    def rcopy(eng, out, in_, reads, writes):
        if eng == S:
            P.op(S, "activation", out, in_, AF.Copy, reads=reads, writes=writes)
        else:
            P.op(eng, "tensor_copy", out, in_, reads=reads, writes=writes)

    cols = P.sb("cols", [128, NCOLS])
    P.dma(cols[:], I["cols"][:, :], writes=["cols"])
    ones_f = P.sb("ones_f", [128, 128])
    P.op(V, "memset", ones_f[:], 1.0, writes=["ones_f"])
    ones = P.sb("ones", [128, 128], R32)
    P.op(V, "tensor_copy", ones[:], ones_f[:], reads=["ones_f"], writes=["ones"])
    ident = P.sb("ident", [128, 128])
    P.dma(ident[:], I["ident"][:, :], writes=["ident"])
    cst = P.sb("cst", [128, 4])
    P.op(V, "memset", cst[:, 0:1], LN_EPS, writes=["cst0"])
    P.op(V, "memset", cst[:, 1:2], RMS_EPS, writes=["cst1"])
    P.op(V, "memset", cst[:, 2:3], math.pi / 2, writes=["cst2"])
    P.op(V, "memset", cst[:, 3:4], 0.0, writes=["cst3"])
    CSTK = ["cst0", "cst1", "cst2", "cst3", "cols", "ones", "ident"]

    def col(name, k):
        o, w = COLS[name]
        return cols[:, o + k:o + k + 1]

    def rsin(eng, out, in_, shift, tmp, np_, rk, wk, tk):
        e = eng
        P.op(e, "tensor_scalar", tmp, in_, float(shift), 1.0 / TWO_PI, ALU.add, ALU.mult, reads=rk, writes=tk)
        P.op(e, "tensor_scalar_add", tmp, tmp, MAGIC, reads=tk, writes=tk)
        P.op(e, "tensor_scalar_add", tmp, tmp, -MAGIC, reads=tk, writes=tk)
        P.op(e, "scalar_tensor_tensor", tmp, tmp, -TWO_PI, in_, ALU.mult, ALU.add, reads=tk + rk, writes=tk)
        if shift == 0.0:
            P.op(S, "activation", out, tmp, AF.Sin, reads=tk, writes=wk)
        else:
            P.op(S, "activation", out, tmp, AF.Sin, bias=cst[:np_, 2:3], reads=tk + ["cst2"], writes=wk)

    qi = 0
    for wn, (kin, nout) in WSPEC.items():
        if wn in FFNW:
            continue
        src = I[wn].rearrange("(kc p) (n c) -> n p kc c", p=128, c=128)
        dst = WS[wn].rearrange("n p (kc c) -> n p kc c", c=128)
        for n in range(nout // 128):
            P.dma(dst[n], src[n], writes=[("WS", wn, n)], eng=("sp", "act", "pool")[qi % 3])
            qi += 1
    with P.phase():
        wf = [P.sb("wcf%d" % i, [128, FC, 128]) for i in range(3)]
        wb_ = [P.sb("wcb%d" % i, [128, FC, 128], BF16) for i in range(3)]
        ci_ = 0
        for wn in FFNW:
            kin, nout = WSPEC[wn]
            nk = kin // 128
            src = I[wn].rearrange("(kc p) (n c) -> n p kc c", p=128, c=128)
            dst = WS[wn].rearrange("n p (kc c) -> n p kc c", c=128)
            for n in range(nout // 128):
                b3 = ci_ % 3; ci_ += 1
                P.dma(wf[b3][:, 0:nk, :], src[n], writes=[("wcf", b3)], eng=("sp", "act", "pool")[ci_ % 3])
                rcopy(S if ci_ % 2 else V, wb_[b3][:, 0:nk, :], wf[b3][:, 0:nk, :], [("wcf", b3)], [("wcb", b3)])
                P.dma(dst[n], wb_[b3][:, 0:nk, :], reads=[("wcb", b3)], writes=[("WS", wn, n)], eng=("sp", "act", "pool")[(ci_ + 1) % 3])

    if 'F' not in _SKIP:
      with P.phase():
          fw1 = P.sb("fw1", [17, 64]); fw2 = P.sb("fw2", [64, 64]); fw3 = P.sb("fw3", [64, 2048])
          fcols = P.sb("fcols", [64, 6]); nrate = P.sb("nrate", [1, 2048])
          P.dma(fw1[:], I["fw1"][:, :], writes=["fw1"])
          P.dma(fw2[:], I["fw2"][:, :], writes=["fw2"])
          P.dma(fw3[:], I["fw3"][:, :], writes=["fw3"])
          P.dma(fcols[:, 0:4], I["fcols"][:, :], writes=["fcols"])
          P.dma(nrate[:], I["logdec"][:, :], writes=["nrate"])
          P.op(S, "activation", nrate[:], nrate[:], AF.Exp, reads=["nrate"], writes=["nrate"])
          P.op(V, "tensor_scalar_mul", nrate[:], nrate[:], -1.0, reads=["nrate"], writes=["nrate"])
          P.op(V, "tensor_tensor", fcols[:, 4:5], fcols[:, 0:1], fcols[:, 1:2], ALU.mult, reads=["fcols"], writes=["fc4"])
          P.op(V, "tensor_tensor", fcols[:, 5:6], fcols[:, 2:3], fcols[:, 3:4], ALU.mult, reads=["fcols"], writes=["fc5"])
          Nmax = 2 * max(Lp, Ls)
          H2 = P.sb("H2", [64, Nmax])
          kbuf = P.sb("kbuf", [128, Nmax])
          ft = [P.sb("ft%d" % i, [17, 512]) for i in range(2)]
          tlb = [P.sb("tlb%d" % i, [1, 512]) for i in range(3)]
          fa_ = [P.sb("fa%d" % i, [64, 512]) for i in range(2)]
          ftmp = [P.sb("ftmp%d" % i, [64, 512]) for i in range(2)]
          fh1 = [P.sb("fh1%d" % i, [64, 512]) for i in range(2)]
          dec = [P.sb("dec%d" % i, [128, 512]) for i in range(2)]
          absb = [P.sb("absb%d" % i, [128, 512]) for i in range(2)]
          asum = P.sb("asum", [128, 40])
          psF = [P.ps("psF%d" % i, [128, 512]) for i in range(4)]
          it = 0
          for si, (off, L) in enumerate(seqs):
              N = 2 * L
              ncol = N // 512
              feats = I["c%d_feats" % si]; tl = I["c%d_tl" % si]
              for j in range(ncol):
                  b = it % 2; it += 1
                  cs = slice(j * 512, (j + 1) * 512)
                  P.dma(ft[b][:], feats[:, cs], writes=[("ft", b)])
                  pa = psF[b]
                  P.op(T, "matmul", pa[:64, :], fw1[:, :], ft[b][:], start=True, stop=True,
                       reads=["fw1", ("ft", b)], writes=[("psF", b)])
                  P.op(V, "tensor_scalar", fa_[b][:], pa[:64, :], fcols[:, 0:1], fcols[:, 4:5], ALU.mult, ALU.add,
                       reads=[("psF", b), "fcols", "fc4"], writes=[("fa", b)])
                  rsin(V, fh1[b][:], fa_[b][:], 0.0, ftmp[b][:], 64, [("fa", b)], [("fh1", b)], [("ftmp", b)])
                  pb = psF[2 + b]
                  P.op(T, "matmul", pb[:64, :], fw2[:, :], fh1[b][:], start=True, stop=True,
                       reads=["fw2", ("fh1", b)], writes=[("psF", 2 + b)])
                  P.op(V, "tensor_scalar", fa_[b][:], pb[:64, :], fcols[:, 2:3], fcols[:, 5:6], ALU.mult, ALU.add,
                       reads=[("psF", 2 + b), "fcols", "fc5"], writes=[("fa", b)])
                  rsin(V, H2[:, cs], fa_[b][:], 0.0, ftmp[b][:], 64, [("fa", b)], [("H2", j)], [("ftmp", b)])
              for o in range(2):
                  for cc in range(4):
                      for j in range(ncol):
                          b = it % 2; b3 = it % 3; it += 1
                          cs = slice(j * 512, (j + 1) * 512)
                          dr = 0 if (j * 512) < L else 1
                          n0 = o * 1024 + dr * 512 + cc * 128
                          P.dma(tlb[b3][:], tl[:, cs], writes=[("tlb", b3)])
                          pe_ = psF[b]
                          P.op(T, "matmul", pe_[:, :], nrate[0:1, n0:n0 + 128], tlb[b3][:], start=True, stop=True,
                               reads=["nrate", ("tlb", b3)], writes=[("psF", b)])
                          P.op(S, "activation", dec[b][:], pe_[:, :], AF.Exp, reads=[("psF", b)], writes=[("dec", b)])
                          pk = psF[2 + b]
                          P.op(T, "matmul", pk[:, :], fw3[:, n0:n0 + 128], H2[:, cs], start=True, stop=True,
                               reads=["fw3", ("H2", j)], writes=[("psF", 2 + b)])
                          P.op(V, "tensor_tensor", kbuf[:, cs], pk[:, :], dec[b][:], ALU.mult,
                               reads=[("psF", 2 + b), ("dec", b)], writes=[("kbuf", j)])
                          if j * 512 == L:
                              P.op(V, "memset", kbuf[:, L:L + 1], 0.0, reads=[("kbuf", j)], writes=[("kbuf", j)])
                          P.op(S, "activation", absb[b][:], kbuf[:, cs], AF.Abs,
                               reads=[("kbuf", j)], writes=[("absb", b)])
                          P.op(V, "reduce_sum", asum[:, j:j + 1], absb[b][:], AX.X, reads=[("absb", b)], writes=[("asum", j)])
                      ak = [("asum", j) for j in range(ncol)]
                      P.op(V, "reduce_sum", asum[:, 32:33], asum[:, 0:ncol], AX.X, reads=ak, writes=["asumT"])
                      P.op(V, "tensor_scalar_add", asum[:, 33:34], asum[:, 32:33], FILTER_EPS, reads=["asumT"], writes=["asumE"])
                      P.op(V, "reciprocal", asum[:, 34:35], asum[:, 33:34], reads=["asumE"], writes=["asumR"])
                      kk = [("kbuf", j) for j in range(ncol)]
                      P.op(V, "tensor_scalar_mul", kbuf[:, 0:N], kbuf[:, 0:N], asum[:, 34:35], reads=kk + ["asumR"], writes=kk)
                      P.dma(KT[si][o, cc * 128:(cc + 1) * 128, :], kbuf[:, 0:N], reads=kk, writes=[("KT", si, o, cc)])

    NPW = 14
    sp_ = P.sb("s5par", [128, 4 * 32 + 2 * NPW * 32 + 2 * 9 * 32])
    MAGs = sp_[:, 0:32]; URs = sp_[:, 32:64]; UIs = sp_[:, 64:96]; MAG8 = sp_[:, 96:128]
    def PWR(s): return sp_[:, 128 + 32 * s:128 + 32 * s + 32]
    def PWI(s): return sp_[:, 128 + 32 * NPW + 32 * s:128 + 32 * NPW + 32 * s + 32]
    _lp0 = 128 + 2 * 32 * NPW
    def LPR(e): return sp_[:, _lp0 + 32 * e:_lp0 + 32 * e + 32]
    def LPI(e): return sp_[:, _lp0 + 288 + 32 * e:_lp0 + 288 + 32 * e + 32]

    def discretize(pfx, lre, lim, lst, Fw, mag, cr, ci, tmp):
        k = lambda n: [pfx + n]
        P.op(S, "activation", lst, lst, AF.Exp, reads=k("lst"), writes=k("lst"))
        P.op(V, "tensor_tensor", mag, lre, lst, ALU.mult, reads=k("lre") + k("lst"), writes=k("mag"))
        P.op(S, "activation", mag, mag, AF.Exp, reads=k("mag"), writes=k("mag"))
        P.op(V, "tensor_tensor", lst, lim, lst, ALU.mult, reads=k("lim") + k("lst"), writes=k("lst"))
        rsin(V, ci, lst, 0.0, tmp, 128, k("lst"), k("ci"), k("tmp"))
        rsin(V, cr, lst, math.pi / 2, tmp, 128, k("lst"), k("cr"), k("tmp"))

    if 'S' not in _SKIP:
      with P.phase():
          ls = [P.sb("ls%d" % i, [128, 32]) for i in range(3)]
          ltmp = P.sb("ltmp", [128, 32])
          for i, n in enumerate(("lre", "lim", "lst")):
              P.dma(ls[i][:], I["lam_s"][i], writes=["s_" + n])
          discretize("s_", ls[0][:], ls[1][:], ls[2][:], 32, MAGs, URs, UIs, ltmp[:])
          P.op(V, "tensor_copy", PWR(0), URs, reads=["s_cr"], writes=[("pw", 0)])
          P.op(V, "tensor_copy", PWI(0), UIs, reads=["s_ci"], writes=[("pw", 0)])
          pt = [P.sb("pt%d" % i, [128, 32]) for i in range(2)]
          for s in range(NPW - 1):
              P.op(V, "tensor_tensor", pt[0][:], PWR(s), PWR(s), ALU.mult, reads=[("pw", s)], writes=["pt0"])
              P.op(V, "tensor_tensor", pt[1][:], PWI(s), PWI(s), ALU.mult, reads=[("pw", s)], writes=["pt1"])
              P.op(V, "tensor_tensor", PWR(s + 1), pt[0][:], pt[1][:], ALU.subtract, reads=["pt0", "pt1"], writes=[("pw", s + 1)])
              P.op(V, "tensor_tensor", pt[0][:], PWR(s), PWI(s), ALU.mult, reads=[("pw", s)], writes=["pt0"])
              P.op(V, "tensor_scalar_mul", PWI(s + 1), pt[0][:], 2.0, reads=["pt0"], writes=[("pw", s + 1)])
          P.op(V, "memset", LPR(0), 1.0, writes=[("lp", 0)])
          P.op(V, "memset", LPI(0), 0.0, reads=[("lp", 0)], writes=[("lp", 0)])
          P.op(V, "tensor_tensor", LPR(1), MAGs, URs, ALU.mult, reads=["s_mag", "s_cr"], writes=[("lp", 1)])
          P.op(V, "tensor_tensor", LPI(1), MAGs, UIs, ALU.mult, reads=["s_mag", "s_ci", ("lp", 1)], writes=[("lp", 1)])
          for e in range(1, 8):
              P.op(V, "tensor_tensor", pt[0][:], LPR(e), LPR(1), ALU.mult, reads=[("lp", e), ("lp", 1)], writes=["pt0"])
              P.op(V, "tensor_tensor", pt[1][:], LPI(e), LPI(1), ALU.mult, reads=[("lp", e), ("lp", 1)], writes=["pt1"])
              P.op(V, "tensor_tensor", LPR(e + 1), pt[0][:], pt[1][:], ALU.subtract, reads=["pt0", "pt1"], writes=[("lp", e + 1)])
              P.op(V, "tensor_tensor", pt[0][:], LPR(e), LPI(1), ALU.mult, reads=[("lp", e), ("lp", 1)], writes=["pt0"])
              P.op(V, "tensor_tensor", pt[1][:], LPI(e), LPR(1), ALU.mult, reads=[("lp", e), ("lp", 1)], writes=["pt1"])
              P.op(V, "tensor_tensor", LPI(e + 1), pt[0][:], pt[1][:], ALU.add, reads=["pt0", "pt1", ("lp", e + 1)], writes=[("lp", e + 1)])
          P.op(V, "tensor_tensor", pt[0][:], MAGs, MAGs, ALU.mult, reads=["s_mag"], writes=["pt0"])
          P.op(V, "tensor_tensor", pt[1][:], pt[0][:], pt[0][:], ALU.mult, reads=["pt0"], writes=["pt1"])
          P.op(V, "tensor_tensor", MAG8, pt[1][:], pt[1][:], ALU.mult, reads=["pt1"], writes=["mag8"])
          A = {n: P.sb("pd_%s" % n, [128, 2048]) for n in
               ("lre", "lim", "lst", "mag", "cr", "ci", "tmp", "bre", "bim", "qr", "qi", "t1", "t2")}
          for d in range(2):
              ds_ = slice(d * 2048, (d + 1) * 2048)
              pk = lambda n: ["p_%s" % n]
              for i, n in enumerate(("lre", "lim", "lst")):
                  P.dma(A[n][:], I["lam_pad"][i][:, ds_], writes=pk(n))
              P.dma(A["bre"][:], I["B_pad"][0][:, ds_], writes=pk("bre"))
              P.dma(A["bim"][:], I["B_pad"][1][:, ds_], writes=pk("bim"))
              discretize("p_", A["lre"][:], A["lim"][:], A["lst"][:], 2048, A["mag"][:], A["cr"][:], A["ci"][:], A["tmp"][:])
              P.op(V, "tensor_tensor", A["cr"][:], A["cr"][:], A["mag"][:], ALU.mult, reads=pk("cr") + pk("mag"), writes=pk("cr"))
              P.op(V, "tensor_scalar_add", A["cr"][:], A["cr"][:], -1.0, reads=pk("cr"), writes=pk("cr"))
              P.op(V, "tensor_tensor", A["ci"][:], A["ci"][:], A["mag"][:], ALU.mult, reads=pk("ci") + pk("mag"), writes=pk("ci"))
              P.op(V, "tensor_tensor", A["t1"][:], A["lre"][:], A["lre"][:], ALU.mult, reads=pk("lre"), writes=pk("t1"))
              P.op(V, "tensor_tensor", A["t2"][:], A["lim"][:], A["lim"][:], ALU.mult, reads=pk("lim"), writes=pk("t2"))
              P.op(V, "tensor_tensor", A["mag"][:], A["t1"][:], A["t2"][:], ALU.add, reads=pk("t1") + pk("t2") + pk("mag"), writes=pk("mag"))
              P.op(V, "reciprocal", A["mag"][:], A["mag"][:], reads=pk("mag"), writes=pk("mag"))
              P.op(V, "tensor_tensor", A["t1"][:], A["cr"][:], A["lre"][:], ALU.mult, reads=pk("cr") + pk("lre"), writes=pk("t1"))
              P.op(V, "tensor_tensor", A["t2"][:], A["ci"][:], A["lim"][:], ALU.mult, reads=pk("ci") + pk("lim"), writes=pk("t2"))
              P.op(V, "tensor_tensor", A["qr"][:], A["t1"][:], A["t2"][:], ALU.add, reads=pk("t1") + pk("t2"), writes=pk("qr"))
              P.op(V, "tensor_tensor", A["qr"][:], A["qr"][:], A["mag"][:], ALU.mult, reads=pk("qr") + pk("mag"), writes=pk("qr"))
              P.op(V, "tensor_tensor", A["t1"][:], A["ci"][:], A["lre"][:], ALU.mult, reads=pk("ci") + pk("lre"), writes=pk("t1"))
              P.op(V, "tensor_tensor", A["t2"][:], A["cr"][:], A["lim"][:], ALU.mult, reads=pk("cr") + pk("lim"), writes=pk("t2"))
              P.op(V, "tensor_tensor", A["qi"][:], A["t1"][:], A["t2"][:], ALU.subtract, reads=pk("t1") + pk("t2"), writes=pk("qi"))
              P.op(V, "tensor_tensor", A["qi"][:], A["qi"][:], A["mag"][:], ALU.mult, reads=pk("qi") + pk("mag"), writes=pk("qi"))
              P.op(V, "tensor_tensor", A["t1"][:], A["qr"][:], A["bre"][:], ALU.mult, reads=pk("qr") + pk("bre"), writes=pk("t1"))
              P.op(V, "tensor_tensor", A["t2"][:], A["qi"][:], A["bim"][:], ALU.mult, reads=pk("qi") + pk("bim"), writes=pk("t2"))
              P.op(V, "tensor_tensor", A["lre"][:], A["t1"][:], A["t2"][:], ALU.subtract, reads=pk("t1") + pk("t2") + pk("lre"), writes=pk("lre"))
              P.op(V, "tensor_tensor", A["t1"][:], A["qr"][:], A["bim"][:], ALU.mult, reads=pk("qr") + pk("bim"), writes=pk("t1"))
              P.op(V, "tensor_tensor", A["t2"][:], A["qi"][:], A["bre"][:], ALU.mult, reads=pk("qi") + pk("bre"), writes=pk("t2"))
              P.op(V, "tensor_tensor", A["lim"][:], A["t1"][:], A["t2"][:], ALU.add, reads=pk("t1") + pk("t2") + pk("lim"), writes=pk("lim"))
              P.op(V, "tensor_scalar_add", A["cr"][:], A["cr"][:], 1.0, reads=pk("cr"), writes=pk("cr"))
              BWv = [BBW[r].rearrange("q (d e x) -> q d e x", d=2, e=8) for r in range(2)]
              cur = ("lre", "lim"); nxt = ("qr", "qi")
              for e in range(8):
                  P.dma(BWv[0][:, d, e, :], A[cur[0]][:], reads=pk(cur[0]), writes=[("BBW", 0, d, e)])
                  P.dma(BWv[1][:, d, e, :], A[cur[1]][:], reads=pk(cur[1]), writes=[("BBW", 1, d, e)])
                  if e == 7:
                      break
                  P.op(V, "tensor_tensor", A["t1"][:], A[cur[0]][:], A["cr"][:], ALU.mult, reads=pk(cur[0]) + pk("cr"), writes=pk("t1"))
                  P.op(G_, "tensor_tensor", A["t2"][:], A[cur[1]][:], A["ci"][:], ALU.mult, reads=pk(cur[1]) + pk("ci"), writes=pk("t2"))
                  P.op(V, "tensor_tensor", A[nxt[0]][:], A["t1"][:], A["t2"][:], ALU.subtract, reads=pk("t1") + pk("t2") + pk(nxt[0]), writes=pk(nxt[0]))
                  P.op(V, "tensor_tensor", A["t1"][:], A[cur[0]][:], A["ci"][:], ALU.mult, reads=pk(cur[0]) + pk("ci"), writes=pk("t1"))
                  P.op(G_, "tensor_tensor", A["t2"][:], A[cur[1]][:], A["cr"][:], ALU.mult, reads=pk(cur[1]) + pk("cr"), writes=pk("t2"))
                  P.op(V, "tensor_tensor", A[nxt[1]][:], A["t1"][:], A["t2"][:], ALU.add, reads=pk("t1") + pk("t2") + pk(nxt[1]), writes=pk(nxt[1]))
                  cur, nxt = nxt, cur

    def ln_stat_chunk(xT, xk, tl, ps_s, ps_q, k):
        xTr = tl["xTr"]
        sq = tl["sq"][k % 2]
        P.op(S, "activation", sq[:], xT[:, k, :], AF.Square, reads=[(xk, k)], writes=[("sq", k % 2)])
        rcopy(V if k % 2 else S, xTr[:, k, :], xT[:, k, :], [(xk, k)], [("xTr", k)])
        P.op(T, "matmul", ps_s[:, :], ones[:, :], xTr[:, k, :], start=(k == 0), stop=(k == KC - 1),
             reads=["ones", ("xTr", k)], writes=["ps_s"])
        P.op(T, "matmul", ps_q[:, :], ones[:, :], sq[:], start=(k == 0), stop=(k == KC - 1),
             reads=["ones", ("sq", k % 2)], writes=["ps_q"])

    def ln_inplace(xT, xk, gname, bname, tl, ps_s, ps_q, want_r=True, want_b=False, stats_done=False):
        xTr = tl["xTr"]
        if not stats_done:
            for k in range(KC):
                ln_stat_chunk(xT, xk, tl, ps_s, ps_q, k)
        mean, var = tl["mean"], tl["var"]
        P.op(S, "activation", mean[:], ps_s[:, :], AF.Copy, scale=1.0 / D, reads=["ps_s"], writes=["mean"])
        P.op(V, "tensor_tensor", var[:], mean[:], mean[:], ALU.mult, reads=["mean"], writes=["var"])
        P.op(V, "scalar_tensor_tensor", var[:], ps_q[:, :], 1.0 / D, var[:], ALU.mult, ALU.subtract,
             reads=["ps_q", "var"], writes=["var"])
        P.op(S, "activation", var[:], var[:], AF.Sqrt, bias=cst[:, 0:1], reads=["var", "cst0"], writes=["var"])
        P.op(V, "reciprocal", var[:], var[:], reads=["var"], writes=["var"])
        for k in range(KC):
            e = V if k % 2 == 0 else G_
            P.op(e, "tensor_tensor", xT[:, k, :], xT[:, k, :], mean[:], ALU.subtract, reads=[(xk, k), "mean"], writes=[(xk, k)])
            P.op(e, "tensor_tensor", xT[:, k, :], xT[:, k, :], var[:], ALU.mult, reads=[(xk, k), "var"], writes=[(xk, k)])
            P.op(V, "tensor_scalar", xT[:, k, :], xT[:, k, :], col(gname, k), col(bname, k), ALU.mult, ALU.add,
                 reads=[(xk, k), "cols"], writes=[(xk, k)])
            if want_r:
                rcopy(S, xTr[:, k, :], xT[:, k, :], [(xk, k)], [("xTr", k)])
            if want_b:
                rcopy(G_ if k % 2 else S, tl["xTb"][:, k, :], xT[:, k, :], [(xk, k)], [("xTb", k)])

    class WStream:
        CAP = {"s": 3, "b": 8, "d": 4}

        def __init__(self, tl, seq):
            self.tl = tl; self.seq = seq; self.pos = 0
            self.issued_n = {k: 0 for k in self.CAP}; self.ret_n = {k: 0 for k in self.CAP}
            self.done = [False] * len(seq); self.h = {}
            self.cnt = {"st": 0, "g": 0, "d": 0, "e": 0, "b": 0}

        def _issue(self, i):
            tl = self.tl
            wn, n, nk, kind = self.seq[i]
            c = self.cnt
            view = WS[wn].rearrange("n p (kc c) -> n p kc c", c=128)[n]
            if kind == "s":
                st = c["st"] % 2; c["st"] += 1
                sg = c["g"] % 3; c["g"] += 1
                P.dma(tl["wst"][st][:, 0:nk, :], view, reads=[("WS", wn, n)], writes=[("wst", st)])
                e = (S, V)[c["e"] % 2]; c["e"] += 1
                rcopy(e, tl["wr"][sg][:, 0:nk, :], tl["wst"][st][:, 0:nk, :], [("wst", st)], [("wr", sg)])
                self.h[i] = (tl["wr"][sg], ("wr", sg))
            elif kind == "b":
                sb_ = c["b"] % 8; c["b"] += 1
                P.dma(tl["wb"][sb_][:, 0:nk, :], view, reads=[("WS", wn, n)], writes=[("wb", sb_)])
                self.h[i] = (tl["wb"][sb_], ("wb", sb_))
            else:
                sd = c["d"] % 4; c["d"] += 1
                P.dma(tl["wdb"][sd][:, 0:nk, :], view, reads=[("WS", wn, n)], writes=[("wdb", sd)])
                self.h[i] = (tl["wdb"][sd], ("wdb", sd))
            self.issued_n[kind] += 1
            self.done[i] = True

        def get(self):
            i = self.pos; self.pos += 1
            blocked = set()
            for j in range(i, min(len(self.seq), i + 12)):
                kind = self.seq[j][3]
                if self.done[j] or kind in blocked:
                    continue
                if self.issued_n[kind] - self.ret_n[kind] < self.CAP[kind]:
                    self._issue(j)
                else:
                    blocked.add(kind)
            assert self.done[i]
            self.ret_n[self.seq[i][3]] += 1
            return self.h.pop(i)

    def ffn_seq(wg, wu, wd):
        q = []
        for f in range(FC):
            q.append((wg, f, KC, "b")); q.append((wu, f, KC, "b"))
        for dch in range(KC):
            q.append((wd, dch, FC, "d"))
        return q

    def ffn_inplace(xT, xk, ws_, tl, psG, psU, psD, ps_s=None, ps_q=None):
        hT = tl["hT"]; xTb = tl["xTb"]
        for f in range(FC):
            b = f % 2
            wgt, kg_ = ws_.get()
            for k in range(KC):
                P.op(T, "matmul", psG[b][:, :], wgt[:, k, :], xTb[:, k, :], start=(k == 0), stop=(k == KC - 1),
                     reads=[kg_, ("xTb", k)], writes=[("psG", b)])
            wut, ku_ = ws_.get()
            for k in range(KC):
                P.op(T, "matmul", psU[b][:, :], wut[:, k, :], xTb[:, k, :], start=(k == 0), stop=(k == KC - 1),
                     reads=[ku_, ("xTb", k)], writes=[("psU", b)])
            P.op(S, "activation", tl["s"][b][:], psG[b][:, :], AF.Silu, reads=[("psG", b)], writes=[("s", b)])
            P.op(V, "tensor_tensor", hT[:, f, :], tl["s"][b][:], psU[b][:, :], ALU.mult,
                 reads=[("s", b), ("psU", b)], writes=[("hT", f)])
        for dch in range(KC):
            b = dch % 2
            wdt, kd_ = ws_.get()
            for f in range(FC):
                P.op(T, "matmul", psD[b][:, :], wdt[:, f, :], hT[:, f, :], start=(f == 0), stop=(f == FC - 1),
                     reads=[kd_, ("hT", f)], writes=[("psD", b)])
            P.op(S, "activation", tl["s"][b][:], psD[b][:, :], AF.Copy, scale=0.5, reads=[("psD", b)], writes=[("s", b)])
            if ps_s is not None and dch > 0:
                ln_stat_chunk(xT, xk, tl, ps_s, ps_q, dch - 1)
            P.op(V, "scalar_tensor_tensor", xT[:, dch, :], xT[:, dch, :], ALPHA, tl["s"][b][:], ALU.mult, ALU.add,
                 reads=[(xk, dch), ("s", b)], writes=[(xk, dch)])
        if ps_s is not None:
            ln_stat_chunk(xT, xk, tl, ps_s, ps_q, KC - 1)

    def alloc_row_tiles():
        tl = {}
        tl["xT"] = P.sb("xT", [128, KC, TT])
        tl["xTr"] = P.sb("xTr", [128, KC, TT], R32)
        tl["xTb"] = P.sb("xTb", [128, KC, TT], BF16)
        tl["hT"] = P.sb("hT", [128, FC, TT], BF16)
        tl["gm"] = P.sb("gm", [128, 12, TT], BF16)
        tl["wb"] = [P.sb("wb%d" % i, [128, KC, 128], BF16) for i in range(8)]
        tl["wdb"] = [P.sb("wdb%d" % i, [128, FC, 128], BF16) for i in range(4)]
        tl["s"] = [P.sb("s%d" % i, [128, TT]) for i in range(2)]
        tl["sq"] = [P.sb("sq%d" % i, [128, TT], R32) for i in range(2)]
        tl["mean"] = P.sb("mean", [128, TT]); tl["var"] = P.sb("var", [128, TT])
        tl["stg"] = P.sb("stg", [128, 10, TT])
        psG = [P.ps("psG%d" % i, [128, TT]) for i in range(2)]
        psU = [P.ps("psU%d" % i, [128, TT]) for i in range(2)]
        psD = [P.ps("psD%d" % i, [128, TT]) for i in range(2)]
        ps_s = P.ps("ps_s", [128, TT]); ps_q = P.ps("ps_q", [128, TT])
        return tl, psG, psU, psD, ps_s, ps_q

    X1Tv = X1T.rearrange("(k p) t -> p k t", p=128)

    if 'A' not in _SKIP:
      with P.phase():
          tl, psG, psU, psD, ps_s, ps_q = alloc_row_tiles()
          xT = tl["xT"]; xTr = tl["xTr"]; hT = tl["hT"]; stg = tl["stg"]
          halo = P.sb("halo", [128, 12, 2])
          PB = [P.sb("PB%d" % i, [128, 515]) for i in range(2)]
          CO = [P.sb("CO%d" % i, [128, 515]) for i in range(2)]
          for i in range(2):
              P.op(V, "memset", PB[i][:], 0.0, writes=[("PB", i)])
          winv = I["win"].rearrange("(kc p) (n c) -> n p kc c", p=128, c=128)
          wseq = []
          for ti in range(Ltot // TT):
              wseq += ffn_seq("wg1", "wu1", "wd1") + [("win", c, KC, "b") for c in range(16)]
          ws_ = WStream(tl, wseq)
          USv = US.rearrange("(k p) t -> p k t", p=128)
          xin = P.sb("xin", [128, 8, TT])
          def load_x(tj):
              for sub in range(4):
                  P.dma(xin[:, 2 * sub:2 * sub + 2, :], I["x"][tj * TT + sub * 128:tj * TT + (sub + 1) * 128, :].rearrange("t (a b) -> t a b", b=TT),
                        writes=[("xin", 2 * sub), ("xin", 2 * sub + 1)])
          for ti in range(Ltot // TT):
              t0 = ti * TT
              first = any(t0 == off for off, L in seqs)
              last = any(t0 + TT == off + L for off, L in seqs)
              if ti == 0:
                  load_x(0)
              for k in range(KC):
                  b = k % 2
                  for sub in range(4):
                      P.op(T, "transpose", psD[b][:, sub * 128:(sub + 1) * 128],
                           xin[:, 2 * sub + k // 4, (k % 4) * 128:(k % 4 + 1) * 128], ident[:, :],
                           reads=[("xin", 2 * sub + k // 4), "ident"], writes=[("psD", b)])
                  rcopy(S if k % 2 else V, xT[:, k, :], psD[b][:, :], [("psD", b)], [("xT", k)])
                  rcopy(V if k % 2 else S, tl["xTb"][:, k, :], xT[:, k, :], [("xT", k)], [("xTb", k)])
              if ti + 1 < Ltot // TT:
                  load_x(ti + 1)
              ffn_inplace(xT, "xT", ws_, tl, psG, psU, psD, ps_s, ps_q)
              ln_inplace(xT, "xT", "ln1g", "ln1b", tl, ps_s, ps_q, want_r=False, want_b=True, stats_done=True)
              P.dma(X1Tv[:, :, t0:t0 + TT], xT[:, :, :], reads=[("xT", k) for k in range(KC)], writes=[("X1T", ti)])
              for c in range(16):
                  b = c % 2
                  wt_, kw_ = ws_.get()
                  for k in range(KC):
                      P.op(T, "matmul", psD[b][:, :], wt_[:, k, :], tl["xTb"][:, k, :], start=(k == 0), stop=(k == KC - 1),
                           reads=[kw_, ("xTb", k)], writes=[("psD", b)])
                  if c < 4:
                      P.op(S, "activation", stg[:, c, :], psD[b][:, :], AF.Copy, reads=[("psD", b)], writes=[("stg", c)])
                      if c == 3:
                          P.dma(USv[:, :, t0:t0 + TT], stg[:, 0:4, :], reads=[("stg", q) for q in range(4)], writes=[("US", ti)])
                      continue
                  hc = c - 4
                  pb = PB[hc % 2]; co = CO[hc % 2]
                  kb = ("PB", hc % 2); kc_ = ("CO", hc % 2)
                  P.op(S, "activation", pb[:, 2:514], psD[b][:, :], AF.Copy, reads=[("psD", b)], writes=[kb])
                  if first:
                      P.op(G_, "memset", pb[:, 0:2], 0.0, reads=[kb], writes=[kb])
                  else:
                      P.op(G_, "tensor_copy", pb[:, 0:2], halo[:, hc, :], reads=[kb, ("halo", hc)], writes=[kb])
                  lo = 2 if first else 1
                  hi = 513 if last else 512
                  n_ = hi - lo + 1
                  P.op(V, "tensor_scalar", co[:, lo:lo + n_], pb[:, lo - 1:lo - 1 + n_], col("sw0", hc), col("sb", hc),
                       ALU.mult, ALU.add, reads=[kb, "cols"], writes=[kc_])
                  P.op(V, "scalar_tensor_tensor", co[:, lo:lo + n_], pb[:, lo:lo + n_], col("sw1", hc), co[:, lo:lo + n_],
                       ALU.mult, ALU.add, reads=[kb, kc_, "cols"], writes=[kc_])
                  P.op(V, "scalar_tensor_tensor", co[:, lo:lo + n_], pb[:, lo + 1:lo + 1 + n_], col("sw2", hc), co[:, lo:lo + n_],
                       ALU.mult, ALU.add, reads=[kb, kc_, "cols"], writes=[kc_])
                  P.op(G_, "tensor_copy", halo[:, hc, :], pb[:, 512:514], reads=[kb], writes=[("halo", hc)])
                  dst = VS[hc // 4]
                  r0 = (hc % 4) * 128
                  P.dma(dst[r0:r0 + 128, t0 + lo - 2:t0 + lo - 2 + n_], co[:, lo:lo + n_], reads=[kc_], writes=[("VS", hc, ti)])

    W = TT
    TAU = 8
    if 'B1' not in _SKIP:
      with P.phase():
        UTf = [P.sb("UTf%d" % i, [128, Ltot // 8]) for i in range(2)]
        UT = P.sb("UT", [128, TAU, Ltot // TAU], R32)
        yacc = P.sb("yacc", [128, Ltot])
        wbf = P.sb("wbf", [128, 2, 8, 128]); wbr = P.sb("wbr", [128, 2, 8, 128], R32)
        cpf = P.sb("cpf", [128, 2, 2, 128])
        dpf = P.sb("dpf", [128, 128]); dpd = P.sb("dpd", [128, 128], R32)
        VC = P.sb("VC", [128, 2, 9, 2, 128], R32)
        MI = P.sb("MI", [128, 2, 8, 128], R32)
        bT = P.sb("bT", [128, 2, 128], R32)
        vt = [P.sb("vt%d" % i, [128, 128]) for i in range(2)]
        Tr = [P.sb("Tr%d" % d, [128, W]) for d in range(2)]
        Ti = [P.sb("Ti%d" % d, [128, W]) for d in range(2)]
        NB = 2
        wk = {n: [P.sb("w_%s%d" % (n, i), [128, W]) for i in range(NB)] for n in
              ("t1", "t2", "t3", "t4", "Mr", "Mi", "Gr", "Gi")}
        cry = P.sb("cry", [128, 8])
        nU8r = P.sb("nU8r", [128, 32])
        P.op(V, "tensor_scalar_mul", nU8r[:], PWR(3), -1.0, writes=["nU8r"])
        kbase = []
        kb_ = 0
        for off, L in seqs:
            kbase.append(kb_)
            kb_ += L // TAU + 1
        HB = [[P.sb("HB%d%d" % (d, r), [128, kb_], R32) for r in range(2)] for d in range(2)]
        hz = P.sb("hz", [128, 4])
        P.op(V, "memset", hz[:], 0.0, writes=["hz"])
        for d in range(2):
            for r in range(2):
                for si, (off, L) in enumerate(seqs):
                    zc = kbase[si] if d == 0 else kbase[si] + L // TAU
                    P.op(V, "tensor_copy", HB[d][r][:, zc:zc + 1], hz[:, 0:1], reads=["hz"], writes=[("HBz", d, r, si)])
        psS = [P.ps("psS%d" % i, [128, W]) for i in range(2)]
        psO = P.ps("psO", [128, 4 * W])
        psM = [P.ps("psM%d" % i, [128, 128]) for i in range(2)]
        BWl = [BBW[r].rearrange("q (d e g s) -> q d e g s", d=2, e=8, g=16) for r in range(2)]
        CPv = [I["C_pad"][r].rearrange("s (d g q) -> s d g q", d=2, g=16) for r in range(2)]
        itn = 0
        mcnt = 0
        for ct in range(4):
            PQ = Ltot // 8
            for q4 in range(8):
                uf = UTf[q4 % 2]
                P.dma(uf[:, :], US[ct * 128:(ct + 1) * 128, q4 * PQ:(q4 + 1) * PQ], writes=[("UTf", q4 % 2)])
                for s8 in range(TAU):
                    rcopy((S, G_, V)[s8 % 3], UT[:, s8, q4 * PQ // TAU:(q4 + 1) * PQ // TAU], uf[:, s8:PQ - TAU + s8 + 1:TAU],
                          [("UTf", q4 % 2)], ["UT"])
            P.dma(dpf[:, :], I["D_pad"][:, ct * 128:(ct + 1) * 128], writes=["dpf"])
            rcopy(S, dpd[:, :], dpf[:, :], ["dpf"], ["dpd"])
            for G4 in range(4):
                G = ct * 4 + G4
                for r in range(2):
                    P.dma(cpf[:, r, :, :], CPv[r][:, :, G, :], writes=[("cpf", r)])
                for d in range(2):
                    dG = d * 16 + G
                    kwb = "wbr"
                    for r in range(2):
                        P.dma(wbf[:, r, :, :], BWl[r][:, d, :, G, :], writes=[("wbf", r)])
                        rcopy(S, wbr[:, r, :, :], wbf[:, r, :, :], [("wbf", r)], [kwb])
                    kvc = ("VC", d)
                    for e in range(9):
                        lr = LPR(e)[:, dG:dG + 1]; li = LPI(e)[:, dG:dG + 1]
                        P.op(V, "tensor_scalar_mul", vt[0][:], cpf[:, 1, d, :], li, reads=[("cpf", 1)], writes=["vt0"])
                        P.op(V, "scalar_tensor_tensor", VC[:, d, e, 0, :], cpf[:, 0, d, :], lr, vt[0][:], ALU.mult, ALU.subtract,
                             reads=[("cpf", 0), "vt0"], writes=[kvc])
                        P.op(V, "tensor_scalar_mul", vt[1][:], cpf[:, 1, d, :], lr, reads=[("cpf", 1)], writes=["vt1"])
                        P.op(V, "scalar_tensor_tensor", VC[:, d, e, 1, :], cpf[:, 0, d, :], li, vt[1][:], ALU.mult, ALU.add,
                             reads=[("cpf", 0), "vt1", kvc], writes=[kvc])
                    for r in range(2):
                        pm = psM[mcnt % 2]; kpm = ("psM", mcnt % 2); mcnt += 1
                        P.op(T, "transpose", pm[:, :], wbf[:, r, 0, :], ident[:, :], reads=[("wbf", r), "ident"], writes=[kpm])
                        P.op(S, "activation", bT[:, r, :], pm[:, :], AF.Copy, scale=(1.0 if r == 0 else -1.0), reads=[kpm], writes=[("bT", r)])
                    for lag in range(8):
                        pm = psM[mcnt % 2]; kpm = ("psM", mcnt % 2); mcnt += 1
                        P.op(T, "matmul", pm[:, :], bT[:, 0, :], VC[:, d, lag, 0, :], start=True, stop=False,
                             reads=[("bT", 0), kvc], writes=[kpm])
                        P.op(T, "matmul", pm[:, :], bT[:, 1, :], VC[:, d, lag, 1, :], start=False, stop=True,
                             reads=[("bT", 1), kvc], writes=[kpm])
                        rcopy(S if lag % 2 else V, MI[:, d, lag, :], pm[:, :], [kpm], [("MI", d)])
                    tk = ("T", d)
                    P.op(V, "memset", Tr[d][:, 0:1], 1.0, writes=[tk])
                    P.op(V, "memset", Ti[d][:, 0:1], 0.0, reads=[tk], writes=[tk])
                    n = 1; s_ = 3
                    while n < W:
                        pr = PWR(s_)[:, dG:dG + 1]; pi_ = PWI(s_)[:, dG:dG + 1]
                        t1 = wk["t1"][0]; t2 = wk["t2"][0]
                        P.op(V, "tensor_scalar_mul", t1[:, 0:n], Ti[d][:, 0:n], pi_, reads=[tk], writes=[("t1", 0)])
                        P.op(V, "tensor_scalar_mul", t2[:, 0:n], Tr[d][:, 0:n], pi_, reads=[tk], writes=[("t2", 0)])
                        P.op(V, "scalar_tensor_tensor", Tr[d][:, n:2 * n], Tr[d][:, 0:n], pr, t1[:, 0:n], ALU.mult, ALU.subtract,
                             reads=[tk, ("t1", 0)], writes=[tk])
                        P.op(V, "scalar_tensor_tensor", Ti[d][:, n:2 * n], Ti[d][:, 0:n], pr, t2[:, 0:n], ALU.mult, ALU.add,
                             reads=[tk, ("t2", 0)], writes=[tk])
                        n *= 2; s_ += 1
                    for si, (off, L) in enumerate(seqs):
                        Kc = L // TAU
                        Wk = min(W, Kc)
                        nw = Kc // Wk
                        for wi in range(nw):
                            w_ = wi if d == 0 else nw - 1 - wi
                            k0 = w_ * Wk
                            b = itn % NB; itn += 1
                            pr_, pi2 = psS[0], psS[1]
                            kpr, kpi = ("psS", 0), ("psS", 1)
                            for r, (pp, kk) in enumerate(((pr_, kpr), (pi2, kpi))):
                                for s8 in range(TAU):
                                    e = (TAU - 1 - s8) if d == 0 else s8
                                    c_a = off // TAU + k0
                                    P.op(T, "matmul", pp[:, 0:Wk], wbr[:, r, e, :], UT[:, s8, c_a:c_a + Wk],
                                         start=(s8 == 0), stop=(s8 == TAU - 1), reads=[kwb, "UT"], writes=[kk])
                            br = pr_[:, 0:Wk] if d == 0 else pr_[:, Wk - 1::-1] if True else None
                            bi = pi2[:, 0:Wk] if d == 0 else pi2[:, Wk - 1::-1]
                            K = lambda n_: (n_, b)
                            A_ = lambda n_: wk[n_][b][:, 0:Wk]
                            TR = Tr[d][:, 0:Wk]; TI = Ti[d][:, 0:Wk]
                            P.op(V, "tensor_tensor", A_("t1"), br, TR, ALU.mult, reads=[kpr, tk], writes=[K("t1")])
                            P.op(V, "tensor_tensor", A_("t2"), bi, TI, ALU.mult, reads=[kpi, tk], writes=[K("t2")])
                            P.op(V, "tensor_tensor", A_("t3"), bi, TR, ALU.mult, reads=[kpi, tk], writes=[K("t3")])
                            P.op(V, "tensor_tensor", A_("t4"), br, TI, ALU.mult, reads=[kpr, tk], writes=[K("t4")])
                            P.op(G_, "tensor_tensor", A_("Mr"), A_("t1"), A_("t2"), ALU.add, reads=[K("t1"), K("t2")], writes=[K("Mr")])
                            P.op(G_, "tensor_tensor", A_("Mi"), A_("t3"), A_("t4"), ALU.subtract, reads=[K("t3"), K("t4")], writes=[K("Mi")])
                            mg = MAG8[:, dG:dG + 1]
                            if wi == 0:
                                ir, ii = 0.0, 0.0
                                rk = []
                            else:
                                ir, ii = cry[:, 0:1], cry[:, 1:2]
                                rk = ["cry"]
                            P.op(V, "tensor_tensor_scan", A_("Gr"), mg.to_broadcast([128, Wk]), A_("Mr"), ir, ALU.mult, ALU.add,
                                 reads=[K("Mr"), "mag8"] + rk, writes=[K("Gr")])
                            P.op(V, "tensor_tensor_scan", A_("Gi"), mg.to_broadcast([128, Wk]), A_("Mi"), ii, ALU.mult, ALU.add,
                                 reads=[K("Mi"), "mag8"] + rk, writes=[K("Gi")])
                            c0 = kbase[si] + k0 + (1 if d == 0 else 0)
                            if d == 0:
                                ho_r = HB[d][0][:, c0:c0 + Wk]; ho_i = HB[d][1][:, c0:c0 + Wk]
                                lastc = c0 + Wk - 1
                            else:
                                ho_r = HB[d][0][:, c0 + Wk - 1:(c0 - 1 if c0 > 0 else None):-1]
                                ho_i = HB[d][1][:, c0 + Wk - 1:(c0 - 1 if c0 > 0 else None):-1]
                                lastc = c0
                            kh = ("HB", d, si, w_)
                            P.op(V, "tensor_tensor", A_("t1"), A_("Gr"), TR, ALU.mult, reads=[K("Gr"), tk], writes=[K("t1")])
                            P.op(V, "tensor_tensor", A_("t2"), A_("Gi"), TI, ALU.mult, reads=[K("Gi"), tk], writes=[K("t2")])
                            P.op(G_, "tensor_tensor", A_("t3"), A_("Gr"), TI, ALU.mult, reads=[K("Gr"), tk], writes=[K("t3")])
                            P.op(G_, "tensor_tensor", A_("t4"), A_("Gi"), TR, ALU.mult, reads=[K("Gi"), tk], writes=[K("t4")])
                            P.op(G_, "tensor_tensor", ho_r, A_("t1"), A_("t2"), ALU.subtract, reads=[K("t1"), K("t2")], writes=[kh])
                            P.op(V, "scalar_tensor_tensor", ho_i, A_("t3"), -1.0, A_("t4"), ALU.mult, ALU.subtract,
                                 reads=[K("t3"), K("t4"), kh], writes=[kh])
                            if wi < nw - 1:
                                ur = PWR(3)[:, dG:dG + 1]; ui = PWI(3)[:, dG:dG + 1]; nur = nU8r[:, dG:dG + 1]
                                hl = f32(HB[d][0][:, lastc:lastc + 1]); nhl = f32(HB[d][1][:, lastc:lastc + 1])
                                P.op(V, "tensor_scalar_mul", cry[:, 2:3], hl, ur, reads=[kh], writes=["cry2"])
                                P.op(V, "tensor_scalar_mul", cry[:, 3:4], hl, ui, reads=[kh], writes=["cry3"])
                                P.op(V, "scalar_tensor_tensor", cry[:, 0:1], nhl, ui, cry[:, 2:3], ALU.mult, ALU.add,
                                     reads=[kh, "cry2", "cry"], writes=["cry"])
                                P.op(V, "scalar_tensor_tensor", cry[:, 1:2], nhl, nur, cry[:, 3:4], ALU.mult, ALU.add,
                                     reads=[kh, "cry3", "nU8r", "cry"], writes=["cry"])
                for si, (off, L) in enumerate(seqs):
                    Kc = L // TAU
                    NKB = min(256, Kc)
                    nwin = max(1, Kc // min(W, Kc))
                    hk = [("HB", d, si, w) for d in range(2) for w in range(nwin)] + [("HBz", d, r, si) for d in range(2) for r in range(2)]
                    for kb0 in range(0, Kc, NKB):
                        kc0 = kbase[si] + kb0
                        ku0 = off // TAU + kb0
                        po = psO; kpo = "psO"
                        for t in range(TAU):
                            ov = po[:, t * NKB:(t + 1) * NKB]
                            mm = []
                            mm.append((VC[:, 0, t + 1, 0, :], HB[0][0][:, kc0:kc0 + NKB], [("VC", 0)] + hk))
                            mm.append((VC[:, 0, t + 1, 1, :], HB[0][1][:, kc0:kc0 + NKB], [("VC", 0)] + hk))
                            mm.append((VC[:, 1, TAU - t, 0, :], HB[1][0][:, kc0 + 1:kc0 + 1 + NKB], [("VC", 1)] + hk))
                            mm.append((VC[:, 1, TAU - t, 1, :], HB[1][1][:, kc0 + 1:kc0 + 1 + NKB], [("VC", 1)] + hk))
                            for s8 in range(TAU):
                                uv = UT[:, s8, ku0:ku0 + NKB]
                                if s8 <= t:
                                    mm.append((MI[:, 0, t - s8, :], uv, [("MI", 0), "UT"]))
                                if s8 >= t:
                                    mm.append((MI[:, 1, s8 - t, :], uv, [("MI", 1), "UT"]))
                                if s8 == t and G4 == 0:
                                    mm.append((dpd[:, :], uv, ["dpd", "UT"]))
                            if "S5_nostate" in _SKIP:
                                mm = mm[4:]
                            if "S5_nointra" in _SKIP:
                                mm = mm[:4]
                            for mi_, (lh, rh, rk_) in enumerate(mm):
                                P.op(T, "matmul", ov, lh, rh, start=(mi_ == 0), stop=(mi_ == len(mm) - 1), reads=rk_, writes=[kpo])
                        t0 = off + TAU * kb0
                        ya = yacc[:, t0:t0 + TAU * NKB].rearrange("q (k t) -> q k t", t=TAU)
                        pov = po[:, 0:TAU * NKB].rearrange("q (t k) -> q k t", t=TAU)
                        ky = ("yacc", t0)
                        if G4 == 0:
                            P.op(V, "tensor_copy", ya, pov, reads=[kpo], writes=[ky])
                        else:
                            P.op(V, "tensor_tensor", ya, ya, pov, ALU.add, reads=[kpo, ky], writes=[ky])
            P.dma(YS[ct * 128:(ct + 1) * 128, :], yacc[:, :], reads=[("yacc", t) for t in range(0, Ltot, TAU)], writes=[("YS", ct)])
        if debug:
            dbg_sp = nc.dram_tensor("dbg_sp", [128, 1600], F32, kind="ExternalOutput").ap()
            dbg_hb = nc.dram_tensor("dbg_hb", [4, 128, kb_], F32, kind="ExternalOutput").ap()
            P.dma(dbg_sp[:, :], sp_[:, :], reads=[])
            hbt = P.sb("hbt", [128, kb_])
            for d in range(2):
                for r in range(2):
                    allk = [("HB", d, si, w) for si in range(2) for w in range(4)] + [("HBz", d, r, si) for si in range(2)]
                    P.op(V, "tensor_copy", hbt[:], f32(HB[d][r][:, :]), reads=allk, writes=["hbt"])
                    P.dma(dbg_hb[d * 2 + r], hbt[:], reads=["hbt"], writes=[("dbg_hb", d, r)])

    if 'B2' not in _SKIP:
      with P.phase():
        hbias = P.sb("hbias", [128, 1024])
        P.dma(hbias[:], I["hbias"][:, :], writes=["hbias"])
        psH = [P.ps("psH%d" % i, [128, 512]) for i in range(8)]
        pcnt = [0]
        def slot():
            i = pcnt[0] % 8; pcnt[0] += 1
            return psH[i], ("psH", i)
        KG = 4
        for si, (off, L) in enumerate(seqs):
          with P.phase():
              N, N1, N2 = fft_dims(L)
              Hh = N1 // 2
              CB = 256 // N1
              cn = lambda n, si=si: "c%d_%s" % (si, n)
              C_ = {}
              for n, shp in (("F1", [N1, 2 * N1]), ("F2re", [N2, N2]), ("F2im", [N2, N2]), ("nF2im", [N2, N2]),
                             ("G2a", [N2, 2 * N2]), ("G2b", [N2, 2 * N2]), ("G1re", [N1, Hh]), ("nG1im", [N1, Hh]),
                             ("TWrr", [N2, 2 * N1]), ("TWis", [N2, 2 * N1]), ("TcRR", [N1, 2 * N2]), ("TcIS", [N1, 2 * N2])):
                  if n.startswith("T"):
                      C_[n] = P.sb("hc%d_%s" % (si, n), shp)
                      P.dma(C_[n][:], I[cn(n)][:, :], writes=[cn(n)])
                  else:
                      stg_ = P.sb("hcs%d_%s" % (si, n), shp)
                      P.dma(stg_[:], I[cn(n)][:, :], writes=[cn(n) + "f"])
                      C_[n] = P.sb("hcr%d_%s" % (si, n), shp, R32)
                      rcopy(S, C_[n][:], stg_[:], [cn(n) + "f"], [cn(n)])
              def tiles(n, shp, dt=F32):
                  return [P.sb("h%d_%s%d" % (si, n, i), shp, dt) for i in range(KG)]
              kf = tiles("kf", [N1, CB, N2]); vv = tiles("v", [Hh, CB, N2]); gg = tiles("g", [Hh, CB, N2])
              kfr = tiles("kfr", [N1, CB, N2], R32); dr = tiles("dr", [Hh, CB, N2], R32)
              Atk = tiles("Atk", [N2, CB, 2, N1], R32); Atx = tiles("Atx", [N2, CB, 2, N1], R32)
              TW_ = 2 * CB * max(N1, N2)
              t1k = tiles("t1k", [128, TW_]); t2k = tiles("t2k", [128, TW_]); t1x = tiles("t1x", [128, TW_]); t2x = tiles("t2x", [128, TW_])
              Ks = tiles("Ks", [N2, 2, CB, N1]); Ys = tiles("Ys", [N2, 2, CB, N1], R32); Zt = tiles("Zt", [N1, CB, 2, N2], R32)
              zz = tiles("z", [Hh, CB, N2]); tb = tiles("tb", [Hh, CB, N2])
              twr = C_["TWrr"][:, :].rearrange("p (h f) -> p h f", h=2).unsqueeze(1).to_broadcast([N2, CB, 2, N1])
              twi = C_["TWis"][:, :].rearrange("p (h f) -> p h f", h=2).unsqueeze(1).to_broadcast([N2, CB, 2, N1])
              tcr = C_["TcRR"][:, :].rearrange("p (h f) -> p h f", h=2).unsqueeze(1).to_broadcast([N1, CB, 2, N2])
              tci = C_["TcIS"][:, :].rearrange("p (h f) -> p h f", h=2).unsqueeze(1).to_broadcast([N1, CB, 2, N2])

              def conv_group(c0, b, si=si, off=off, L=L, N1=N1, N2=N2, Hh=Hh, CB=CB):
                  tsl = slice(off, off + L)
                  def ld(dst, srcap, key):
                      P.dma(dst, srcap.rearrange("c (a b) -> a c b", b=N2), writes=[key])
                  def f1(src, srck, Kdim):
                      p1, k1 = slot()
                      p1v = p1[:N2, :].rearrange("p (c h f) -> p c h f", c=CB, h=2)
                      for c in range(CB):
                          P.op(T, "matmul", p1[:N2, c * 2 * N1:(c + 1) * 2 * N1], src[:, c, :], C_["F1"][:Kdim, :], start=True, stop=True,
                               reads=[srck, cn("F1")], writes=[k1])
                      return p1v, k1
                  def twid(p1v, k1, t1_, t2_, at, kt1, kt2, ka, ae=G_):
                      t1v = t1_[:N2, 0:CB * 2 * N1].rearrange("p (c h f) -> p c h f", c=CB, h=2)
                      t2v = t2_[:N2, 0:CB * 2 * N1].rearrange("p (c h f) -> p c h f", c=CB, h=2)
                      P.op(V, "tensor_tensor", t1v, p1v, twr, ALU.mult, reads=[k1, cn("TWrr")], writes=[kt1])
                      P.op(V, "tensor_tensor", t2v, p1v[:, :, ::-1, :], twi, ALU.mult, reads=[k1, cn("TWis")], writes=[kt2])
                      P.op(ae, "tensor_tensor", at[:], t1v, t2v, ALU.add, reads=[kt1, kt2], writes=[ka])
                  def f2(at, ka):
                      p2, k2 = slot()
                      p2v = p2[:N2, :].rearrange("p (h c f) -> p h c f", h=2, c=CB)
                      ar = at[:, :, 0, :]; ai = at[:, :, 1, :]
                      P.op(T, "matmul", p2v[:, 0, :, :], C_["F2re"][:, :], ar, start=True, stop=False, reads=[ka, cn("F2re")], writes=[k2])
                      P.op(T, "matmul", p2v[:, 0, :, :], C_["nF2im"][:, :], ai, start=False, stop=True, reads=[ka, cn("nF2im")], writes=[k2])
                      P.op(T, "matmul", p2v[:, 1, :, :], C_["F2im"][:, :], ar, start=True, stop=False, reads=[ka, cn("F2im")], writes=[k2])
                      P.op(T, "matmul", p2v[:, 1, :, :], C_["F2re"][:, :], ai, start=False, stop=True, reads=[ka, cn("F2re")], writes=[k2])
                      return p2v, k2
                  K_ = lambda n: (n, si, b)
                  ld(vv[b][:], VS[0][c0:c0 + CB, tsl], K_("vv"))
                  zprev, zpk = vv[b], K_("vv")
                  for o in range(2):
                      ld(kf[b][:], KT[si][o, c0:c0 + CB, :], K_("kf"))
                      ld(gg[b][:], VS[1 + o][c0:c0 + CB, tsl], K_("gg"))
                      rcopy(S, kfr[b][:], kf[b][:], [K_("kf")], [K_("kfr")])
                      rcopy(S, dr[b][:], zprev[:], [zpk], [K_("dr")])
                      yield
                      pk1, kk1 = f1(kfr[b], K_("kfr"), N1)
                      px1, kx1 = f1(dr[b], K_("dr"), Hh)
                      yield
                      twid(pk1, kk1, t1k[b], t2k[b], Atk[b], K_("t1k"), K_("t2k"), K_("Atk"), ae=V)
                      twid(px1, kx1, t1x[b], t2x[b], Atx[b], K_("t1x"), K_("t2x"), K_("Atx"))
                      yield
                      pk2, kk2 = f2(Atk[b], K_("Atk"))
                      px, kx = f2(Atx[b], K_("Atx"))
                      yield
                      P.op(S, "activation", Ks[b][:], pk2, AF.Copy, reads=[kk2], writes=[K_("Ks")])
                      t1v = t1k[b][:N2, 0:2 * CB * N1].rearrange("p (h c f) -> p h c f", h=2, c=CB)
                      t2v = t2k[b][:N2, 0:2 * CB * N1].rearrange("p (h c f) -> p h c f", h=2, c=CB)
                      kr = Ks[b][:, 0:1, :, :].to_broadcast([N2, 2, CB, N1])
                      ki = Ks[b][:, 1:2, :, :].to_broadcast([N2, 2, CB, N1])
                      P.op(V, "tensor_tensor", t1v, px, kr, ALU.mult, reads=[kx, K_("Ks")], writes=[K_("t1k")])
                      P.op(V, "tensor_tensor", t2v, px[:, ::-1, :, :], ki, ALU.mult, reads=[kx, K_("Ks")], writes=[K_("t2k")])
                      P.op(G_, "tensor_tensor", Ys[b][:, 0, :, :], t1v[:, 0, :, :], t2v[:, 0, :, :], ALU.subtract,
                           reads=[K_("t1k"), K_("t2k")], writes=[K_("Ys")])
                      P.op(G_, "tensor_tensor", Ys[b][:, 1, :, :], t1v[:, 1, :, :], t2v[:, 1, :, :], ALU.add,
                           reads=[K_("t1k"), K_("t2k"), K_("Ys")], writes=[K_("Ys")])
                      yield
                      p3, k3 = slot()
                      p3v = p3[:N1, :].rearrange("p (c h f) -> p c h f", c=CB, h=2)
                      for c in range(CB):
                          P.op(T, "matmul", p3[:N1, c * 2 * N2:(c + 1) * 2 * N2], Ys[b][:, 0, c, :], C_["G2a"][:, :], start=True, stop=False,
                               reads=[K_("Ys"), cn("G2a")], writes=[k3])
                          P.op(T, "matmul", p3[:N1, c * 2 * N2:(c + 1) * 2 * N2], Ys[b][:, 1, c, :], C_["G2b"][:, :], start=False, stop=True,
                               reads=[K_("Ys"), cn("G2b")], writes=[k3])
                      yield
                      u1 = t1x[b][:N1, 0:CB * 2 * N2].rearrange("p (c h f) -> p c h f", c=CB, h=2)
                      u2 = t2x[b][:N1, 0:CB * 2 * N2].rearrange("p (c h f) -> p c h f", c=CB, h=2)
                      P.op(V, "tensor_tensor", u1, p3v, tcr, ALU.mult, reads=[k3, cn("TcRR")], writes=[K_("t1x")])
                      P.op(V, "tensor_tensor", u2, p3v[:, :, ::-1, :], tci, ALU.mult, reads=[k3, cn("TcIS")], writes=[K_("t2x")])
                      P.op(V if (c0 // CB) % 2 else G_, "tensor_tensor", Zt[b][:], u1, u2, ALU.add, reads=[K_("t1x"), K_("t2x")], writes=[K_("Zt")])
                      yield
                      p4, k4 = slot()
                      p4v = p4[:Hh, 0:CB * N2].rearrange("p (c f) -> p c f", c=CB)
                      P.op(T, "matmul", p4v, C_["G1re"][:, :], Zt[b][:, :, 0, :], start=True, stop=False,
                           reads=[K_("Zt"), cn("G1re")], writes=[k4])
                      P.op(T, "matmul", p4v, C_["nG1im"][:, :], Zt[b][:, :, 1, :], start=False, stop=True,
                           reads=[K_("Zt"), cn("nG1im")], writes=[k4])
                      yield
                      bia = hbias[:Hh, o * 512 + c0:o * 512 + c0 + CB].unsqueeze(2).to_broadcast([Hh, CB, N2])
                      P.op(G_, "tensor_tensor", tb[b][:], zprev[:], bia, ALU.mult, reads=[zpk, "hbias"], writes=[K_("tb")])
                      P.op(V, "tensor_tensor", tb[b][:], tb[b][:], p4v, ALU.add, reads=[K_("tb"), k4], writes=[K_("tb")])
                      P.op(G_, "tensor_tensor", zz[b][:], tb[b][:], gg[b][:], ALU.mult, reads=[K_("tb"), K_("gg"), zpk], writes=[K_("zz")])
                      if o == 0:
                          P.op(S, "activation", vv[b][:], zz[b][:], AF.Copy, reads=[K_("zz")], writes=[K_("vv")])
                          zprev, zpk = vv[b], K_("vv")
                      yield
                  P.dma(YH[c0:c0 + CB, tsl].rearrange("c (a b) -> a c b", b=N2), zz[b][:], reads=[K_("zz")], writes=[("YH", si, c0)])

              pending = list(range(0, DH, CB))
              active = []
              free_b = list(range(KG))
              while pending or active:
                  while pending and free_b:
                      bb_ = free_b.pop(0)
                      active.append((conv_group(pending.pop(0), bb_), bb_))
                  for ent in list(active):
                      try:
                          next(ent[0])
                      except StopIteration:
                          active.remove(ent)
                          free_b.append(ent[1])

    if 'C' not in _SKIP:
      with P.phase():
          tl, psG, psU, psD, ps_s, ps_q = alloc_row_tiles()
          xT = tl["xT"]; hT = tl["hT"]; xTr = tl["xTr"]; stg = tl["stg"]; gm = tl["gm"]
          YSv = YS.rearrange("(k p) t -> p k t", p=128)
          YHv = YH.rearrange("(k p) t -> p k t", p=128)
          gluv = I["gluw"].rearrange("(kc p) (n c) -> n p kc c", p=128, c=128)
          woutv = I["wout"].rearrange("(kc p) (n c) -> n p kc c", p=128, c=128)
          rs = [P.sb("rs%d" % i, [128, TT]) for i in range(2)]
          wseq = []
          for ti in range(Ltot // TT):
              wseq += [("gluw", o, 4, "b") for o in range(4)] + [("wout", dch, KC, "b") for dch in range(KC)] + ffn_seq("wg2", "wu2", "wd2")
          ws_ = WStream(tl, wseq)
          yin = P.sb("yin", [128, 8, TT]); x1in = P.sb("x1in", [128, 8, TT])
          def load_c(tj):
              P.dma(yin[:, 0:4, :], YSv[:, :, tj * TT:(tj + 1) * TT], writes=[("yin", f) for f in range(0, 4)])
              P.dma(yin[:, 4:8, :], YHv[:, :, tj * TT:(tj + 1) * TT], writes=[("yin", f) for f in range(4, 8)])
          def load_x1(tj):
              P.dma(x1in[:, :, :], X1Tv[:, :, tj * TT:(tj + 1) * TT], writes=[("x1in", k) for k in range(KC)])
          load_c(0); load_x1(0)
          for ti in range(Ltot // TT):
              t0 = ti * TT
              for k in range(4):
                  ys = yin[:, k, :]; g = gm[:, k, :]; tm = stg[:, 8 + (k % 2), :]
                  kys, kg, ktm = ("yin", k), ("gm", k), ("stg", 8 + (k % 2))
                  P.op(S, "activation", tm, ys, AF.Square, reads=[kys], writes=[ktm])
                  P.op(V, "tensor_scalar", tm, tm, 0.044715, 1.0, ALU.mult, ALU.add, reads=[ktm], writes=[ktm])
                  P.op(V, "tensor_tensor", tm, tm, ys, ALU.mult, reads=[ktm, kys], writes=[ktm])
                  P.op(S, "activation", tm, tm, AF.Sigmoid, scale=1.5957691216057308, reads=[ktm], writes=[ktm])
                  P.op(G_, "tensor_tensor", ys, ys, tm, ALU.mult, reads=[kys, ktm], writes=[kys])
                  rcopy(S, g, ys, [kys], [kg])
              for o in range(4):
                  b = o % 2
                  wt_, kw_ = ws_.get()
                  for k in range(4):
                      P.op(T, "matmul", psD[b][:, :], wt_[:, k, :], gm[:, k, :], start=(k == 0), stop=(k == 3),
                           reads=[kw_, ("gm", k)], writes=[("psD", b)])
                  P.op(S, "activation", tl["s"][b][:], psD[b][:, :], AF.Sigmoid, bias=col("glub", o),
                       reads=[("psD", b), "cols"], writes=[("s", b)])
                  P.op(V, "tensor_tensor", yin[:, o, :], yin[:, o, :], tl["s"][b][:], ALU.mult,
                       reads=[("yin", o), ("s", b)], writes=[("yin", o)])
              for hgi, (base, gn, pst, kps) in enumerate(((0, "sng", ps_s, "ps_s"), (4, "hng", ps_q, "ps_q"))):
                  for k in range(4):
                      sq = tl["sq"][k % 2]
                      P.op(S, "activation", sq[:], yin[:, base + k, :], AF.Square, reads=[("yin", base + k)], writes=[("sq", k % 2)])
                      P.op(T, "matmul", pst[:, :], ones[:, :], sq[:], start=(k == 0), stop=(k == 3),
                           reads=["ones", ("sq", k % 2)], writes=[kps])
                  P.op(S, "activation", rs[hgi][:], pst[:, :], AF.Sqrt, bias=cst[:, 1:2], scale=1.0 / 512,
                       reads=[kps, "cst1"], writes=[("rs", hgi)])
                  P.op(V, "reciprocal", rs[hgi][:], rs[hgi][:], reads=[("rs", hgi)], writes=[("rs", hgi)])
                  for k in range(4):
                      P.op(V, "scalar_tensor_tensor", gm[:, 4 + base + k, :], yin[:, base + k, :], col(gn, k), rs[hgi][:],
                           ALU.mult, ALU.mult, reads=[("yin", base + k), ("rs", hgi), "cols"], writes=[("gm", 4 + base + k)])
              if ti + 1 < Ltot // TT:
                  load_c(ti + 1)
              for dch in range(KC):
                  b = dch % 2
                  wt_, kw_ = ws_.get()
                  for k in range(KC):
                      P.op(T, "matmul", psD[b][:, :], wt_[:, k, :], gm[:, 4 + k, :], start=(k == 0), stop=(k == KC - 1),
                           reads=[kw_, ("gm", 4 + k)], writes=[("psD", b)])
                  if dch > 0:
                      ln_stat_chunk(xT, "xT", tl, ps_s, ps_q, dch - 1)
                  P.op(V, "scalar_tensor_tensor", xT[:, dch, :], x1in[:, dch, :], ALPHA, psD[b][:, :], ALU.mult, ALU.add,
                       reads=[("x1in", dch), ("psD", b)], writes=[("xT", dch)])
              if ti + 1 < Ltot // TT:
                  load_x1(ti + 1)
              ln_stat_chunk(xT, "xT", tl, ps_s, ps_q, KC - 1)
              ln_inplace(xT, "xT", "ln2g", "ln2b", tl, ps_s, ps_q, want_r=False, want_b=True, stats_done=True)
              ffn_inplace(xT, "xT", ws_, tl, psG, psU, psD, ps_s, ps_q)
              ln_inplace(xT, "xT", "ln3g", "ln3b", tl, ps_s, ps_q, want_r=False, stats_done=True)
              for sub in range(4):
                  for half in range(2):
                      b = half
                      for k4 in range(4):
                          k = half * 4 + k4
                          P.op(T, "transpose", psD[b][:, k4 * 128:(k4 + 1) * 128], xT[:, k, sub * 128:(sub + 1) * 128], ident[:, :],
                               reads=[("xT", k), "ident"], writes=[("psD", b)])
                      rcopy(S if half else V, stg[:, 2 * sub + half, :], psD[b][:, :], [("psD", b)], [("stg", 2 * sub + half)])
                  P.dma(yout[t0 + sub * 128:t0 + (sub + 1) * 128, :].rearrange("t (a b) -> t a b", b=TT), stg[:, 2 * sub:2 * sub + 2, :],
                        reads=[("stg", 2 * sub), ("stg", 2 * sub + 1)], writes=[("yout", ti, sub)])

    P.emit()
    return nc, P


_CACHE = {}


def run(inputs, Lp, Ls, n_cores=8, debug=False):
    in_maps = [host_inputs(inputs, c, Lp, Ls) for c in range(n_cores)]
    shapes = {k: v.shape for k, v in in_maps[0].items()}
    nc, P = build(Lp, Ls, shapes, debug=debug)
    res = run_bass_kernel_spmd(nc, in_maps, core_ids=list(range(n_cores)))
    return res, P


def kernel(**inputs):
    Lp = inputs["x_prompt"].shape[1]
    Ls = inputs["x_sample"].shape[1]
    res, _ = run(inputs, Lp, Ls)
    yp = np.stack([res.results[c]["y"][:Lp] for c in range(8)], 0).astype(np.float32)
    ys = np.stack([res.results[c]["y"][Lp:] for c in range(8)], 0).astype(np.float32)
    return (yp, ys)
```

```python
import math
from contextlib import ExitStack, contextmanager
import numpy as np
import concourse.bass as bass
import concourse.mybir as mybir
from concourse.bass_utils import run_bass_kernel_spmd

F32 = mybir.dt.float32
R32 = mybir.dt.float32r
BF16 = mybir.dt.bfloat16
ALU = mybir.AluOpType
AF = mybir.ActivationFunctionType
AX = mybir.AxisListType

ENGS = ("pe", "act", "dve", "pool", "sp")
_SKIP = set()
D, DFF, DS, DH, DIN = 1024, 2816, 512, 512, 2048
KC, FC = 8, 22
TT = 512
ALPHA = 2.0 ** 0.25
LN_EPS, RMS_EPS, FILTER_EPS = 1e-5, 1e-6, 1e-6
MAGIC = 12582912.0
TWO_PI = 2.0 * math.pi


class Prog:
    def __init__(self, nc, n_dma_sems=8):
        self.nc = nc
        self.root = ExitStack()
        self.stacks = [self.root]
        self.eng = {"pe": nc.tensor, "act": nc.scalar, "dve": nc.vector,
                    "pool": nc.gpsimd, "sp": nc.sync}
        self.ops = []
        self.res = {}
        self.n_dma_sems = n_dma_sems
        self.dma_rr = {e: 0 for e in ENGS}
        self.dma_last = {}
        self.dma_cnt = {}
        self.last_op = {}
        self.uid = 0

    def sb(self, name, shape, dt=F32):
        self.uid += 1
        return self.stacks[-1].enter_context(self.nc.sbuf_tensor("%s_%d" % (name, self.uid), list(shape), dt))

    def ps(self, name, shape, dt=F32):
        self.uid += 1
        return self.stacks[-1].enter_context(self.nc.psum_tensor("%s_%d" % (name, self.uid), list(shape), dt))

    @contextmanager
    def phase(self):
        st = ExitStack()
        self.stacks.append(st)
        try:
            yield
        finally:
            self.barrier()
            self.emit_pending()
            self.stacks.pop()
            st.close()

    def _deps(self, reads, writes):
        deps = set()
        for k in reads:
            st = self.res.get(k)
            if st is not None and st[0] is not None:
                deps.add(st[0])
        for k in writes:
            st = self.res.get(k)
            if st is not None:
                if st[0] is not None:
                    deps.add(st[0])
                deps.update(st[1])
        return deps

    def _commit(self, oid, reads, writes):
        for k in reads:
            st = self.res.setdefault(k, [None, []])
            st[1].append(oid)
        for k in writes:
            self.res[k] = [oid, []]

    def op(self, eng, name, *args, reads=(), writes=(), **kw):
        deps = self._deps(reads, writes)
        oid = len(self.ops)
        self.ops.append(dict(eng=eng, kind="op", name=name, args=args, kw=kw, deps=deps))
        self._commit(oid, reads, writes)
        self.last_op[eng] = oid
        return oid

    def dma(self, out, in_, reads=(), writes=(), eng="sp", **kw):
        deps = self._deps(reads, writes)
        k = self.dma_rr[eng]
        self.dma_rr[eng] = (k + 1) % self.n_dma_sems
        prev = self.dma_last.get((eng, k))
        if prev is not None:
            deps.add(prev)
        cnt = self.dma_cnt.get((eng, k), 0) + 1
        self.dma_cnt[(eng, k)] = cnt
        oid = len(self.ops)
        self.ops.append(dict(eng=eng, kind="dma", out=out, in_=in_, kw=kw, deps=deps,
                             semk=(eng, k), val=16 * cnt))
        self.dma_last[(eng, k)] = oid
        self._commit(oid, reads, writes)
        return oid

    def barrier(self):
        deps = set(self.last_op.values()) | set(self.dma_last.values())
        for e in ENGS:
            oid = len(self.ops)
            self.ops.append(dict(eng=e, kind="bar", deps=set(deps)))
            self.last_op[e] = oid
        self.res = {}

    def emit_pending(self):
        nc = self.nc
        ops = self.ops
        if not hasattr(self, "esem"):
            self.esem = {e: self.root.enter_context(nc.semaphore("es_" + e)) for e in ENGS}
            self.dsem = {}
            self.cnt = {e: 0 for e in ENGS}
            self.sig = []
            self.waited = {e: {} for e in ENGS}
            self.nwait = 0
            self.emitted = 0
        start = self.emitted
        needed = [False] * len(ops)
        for o in ops[start:]:
            for d in o["deps"]:
                needed[d] = True
        esem, dsem, cnt, sig, waited = self.esem, self.dsem, self.cnt, self.sig, self.waited
        sig.extend([None] * (len(ops) - len(sig)))
        for i in range(start, len(ops)):
            o = ops[i]
            e = o["eng"]
            E = self.eng[e]
            kind = o["kind"]
            w = {}
            for d in o["deps"]:
                if sig[d] is None:
                    assert ops[d]["kind"] == "bar" or d >= start, (i, d, ops[d]["kind"])
                    if ops[d]["kind"] != "bar":
                        raise RuntimeError("dependency on unsignalled op")
                    continue
                s_, v = sig[d]
                if ops[d]["eng"] == e and ops[d]["kind"] == "op" and e == "pe" and kind == "op":
                    continue
                if w.get(id(s_), (None, 0))[1] < v:
                    w[id(s_)] = (s_, v)
            for s_, v in w.values():
                if waited[e].get(id(s_), 0) >= v:
                    continue
                waited[e][id(s_)] = v
                E.wait_ge(s_, v)
                self.nwait += 1
            if kind == "bar":
                sig[i] = None
            elif kind == "dma":
                if o["semk"] not in dsem:
                    dsem[o["semk"]] = self.root.enter_context(nc.semaphore("ds_%s_%d" % o["semk"]))
                s_ = dsem[o["semk"]]
                E.dma_start(out=o["out"], in_=o["in_"], **o["kw"]).then_inc(s_, 16)
                sig[i] = (s_, o["val"])
            else:
                ins = getattr(E, o["name"])(*o["args"], **o["kw"])
                if needed[i]:
                    cnt[e] += 1
                    ins.then_inc(esem[e], 1)
                    sig[i] = (esem[e], cnt[e])
            o["args"] = None; o["kw"] = None; o["out"] = None; o["in_"] = None
        self.emitted = len(ops)

    def emit(self):
        self.barrier()
        self.emit_pending()
        nc = self.nc
        for key, s_ in self.dsem.items():
            v = 16 * self.dma_cnt[key]
            if self.waited["sp"].get(id(s_), 0) < v:
                nc.sync.wait_ge(s_, v)
        self.stats = dict(n_ops=len(self.ops), n_wait=self.nwait, cnt=dict(self.cnt))
        self.root.close()


def fft_dims(L):
    N = 2 * L
    lg = int(round(math.log2(N)))
    n1 = 1 << ((lg + 1) // 2)
    n2 = N // n1
    return N, n1, n2


def fft_consts(L):
    N, N1, N2 = fft_dims(L)
    a = np.arange(N1)[:, None]; fa = np.arange(N1)[None, :]
    b = np.arange(N2)[:, None]; fb = np.arange(N2)[None, :]
    w1 = 2 * np.pi * (a * fa % N1) / N1
    w2 = 2 * np.pi * (b * fb % N2) / N2
    c = {}
    c["F1"] = np.concatenate([np.cos(w1), -np.sin(w1)], 1)
    c["F2re"] = np.cos(w2); c["F2im"] = -np.sin(w2); c["nF2im"] = np.sin(w2)
    c["G2a"] = np.concatenate([np.cos(w2), np.sin(w2)], 1)
    c["G2b"] = np.concatenate([-np.sin(w2), np.cos(w2)], 1)
    h = N1 // 2
    c["G1re"] = (np.cos(w1) / N)[:, :h]
    c["nG1im"] = (-np.sin(w1) / N)[:, :h]
    tw = 2 * np.pi * ((np.arange(N2)[:, None] * np.arange(N1)[None, :]) % N) / N
    c["TWrr"] = np.concatenate([np.cos(tw), np.cos(tw)], 1)
    c["TWis"] = np.concatenate([np.sin(tw), -np.sin(tw)], 1)
    twc = tw.T
    c["TcRR"] = np.concatenate([np.cos(twc), np.cos(twc)], 1)
    c["TcIS"] = np.concatenate([-np.sin(twc), np.sin(twc)], 1)
    return {k: np.ascontiguousarray(v, dtype=np.float32) for k, v in c.items()}


def filter_pos_tables(L):
    N = 2 * L
    m = np.arange(N)
    pos = np.where(m < L, m, N - m).astype(np.float32)
    pos[L] = 0.0
    t_lin = (pos / np.float32(4096.0)).astype(np.float32)
    omega = np.exp(np.float32(-math.log(10000.0)) * np.arange(8, dtype=np.float32) / np.float32(8)).astype(np.float32)
    ang = (pos[:, None] * omega[None, :]).astype(np.float32)
    feats = np.concatenate([t_lin[:, None], np.sin(ang), np.cos(ang)], -1).astype(np.float32)
    return np.ascontiguousarray(feats.T), np.ascontiguousarray(t_lin[None, :])


COLS = {}
_off = 0
for _n, _w in (("ln1g", 8), ("ln1b", 8), ("ln2g", 8), ("ln2b", 8), ("ln3g", 8), ("ln3b", 8),
               ("glub", 4), ("sng", 4), ("hng", 4), ("sw0", 12), ("sw1", 12), ("sw2", 12), ("sb", 12)):
    COLS[_n] = (_off, _w)
    _off += _w
NCOLS = _off


def col_layout(v):
    return np.ascontiguousarray(np.asarray(v, np.float32).reshape(-1, 128).T)


def host_inputs(inp, core, Lp, Ls):
    g = lambda k: np.asarray(inp[k], np.float32)[0]
    m = {}
    m["x"] = np.ascontiguousarray(np.concatenate(
        [np.asarray(inp["x_prompt"], np.float32)[core, :Lp], np.asarray(inp["x_sample"], np.float32)[core, :Ls]], 0))
    for k, src in (("wg1", "ffn1_w_gate"), ("wu1", "ffn1_w_up"), ("wd1", "ffn1_w_down"), ("win", "w_in"),
                   ("wout", "w_out"), ("wg2", "ffn2_w_gate"), ("wu2", "ffn2_w_up"), ("wd2", "ffn2_w_down"),
                   ("gluw", "ssm_glu_w"), ("fw1", "hy_filt_w1"), ("fw2", "hy_filt_w2"), ("fw3", "hy_filt_w3")):
        m[k] = np.ascontiguousarray(g(src))
    cols = np.zeros((128, NCOLS), np.float32)
    def put(name, v):
        o, w = COLS[name]
        cols[:, o:o + w] = col_layout(v)
    put("ln1g", g("ln1_g")); put("ln1b", g("ln1_b")); put("ln2g", g("ln2_g")); put("ln2b", g("ln2_b"))
    put("ln3g", g("ln3_g")); put("ln3b", g("ln3_b")); put("glub", g("ssm_glu_b")); put("sng", g("ssm_norm_g"))
    put("hng", g("hy_norm_g"))
    sw = g("hy_short_w")
    put("sw0", sw[0]); put("sw1", sw[1]); put("sw2", sw[2]); put("sb", g("hy_short_b"))
    m["cols"] = cols
    m["hbias"] = np.ascontiguousarray(np.broadcast_to(g("hy_bias").reshape(1, 1024), (128, 1024)))
    sf = g("hy_sin_freq")
    m["fcols"] = np.ascontiguousarray(np.stack([sf[0], g("hy_filt_b1"), sf[1], g("hy_filt_b2")], 1))
    m["logdec"] = np.ascontiguousarray(g("hy_log_decay").reshape(1, 2048))
    lre, lim, lst = g("ssm_lam_re"), g("ssm_lam_im"), g("ssm_log_step")
    bre, bim, cre, cim = g("ssm_b_re"), g("ssm_b_im"), g("ssm_c_re"), g("ssm_c_im")
    lam_pad = np.zeros((3, 128, 2, 16, 128), np.float32)
    B_pad = np.zeros((2, 128, 2, 16, 128), np.float32)
    C_pad = np.zeros((2, 128, 2, 16, 128), np.float32)
    for G in range(16):
        for g2 in range(2):
            gg = 2 * G + g2
            sl = slice(g2 * 64, g2 * 64 + 64)
            lam_pad[0, :, :, G, sl] = lre[:, gg, :][None]
            lam_pad[1, :, :, G, sl] = lim[:, gg, :][None]
            lam_pad[2, :, :, G, sl] = lst[:, gg][None, :, None]
            q0 = (G % 4) * 32 + g2 * 16
            for d in range(2):
                B_pad[0, q0:q0 + 16, d, G, sl] = bre[d, gg].T
                B_pad[1, q0:q0 + 16, d, G, sl] = bim[d, gg].T
                C_pad[0, sl, d, G, q0:q0 + 16] = cre[d, gg].T
                C_pad[1, sl, d, G, q0:q0 + 16] = cim[d, gg].T
    m["lam_pad"] = lam_pad.reshape(3, 128, 4096)
    m["B_pad"] = B_pad.reshape(2, 128, 4096)
    m["C_pad"] = C_pad.reshape(2, 128, 4096)
    lam_s = np.zeros((3, 128, 2, 16), np.float32)
    for G in range(16):
        for g2 in range(2):
            gg = 2 * G + g2
            sl = slice(g2 * 64, g2 * 64 + 64)
            lam_s[0, sl, :, G] = lre[:, gg, :].T
            lam_s[1, sl, :, G] = lim[:, gg, :].T
            lam_s[2, sl, :, G] = lst[:, gg][None, :]
    m["lam_s"] = lam_s.reshape(3, 128, 32)
    dsk = g("ssm_d").reshape(512)
    D_pad = np.zeros((128, 4, 128), np.float32)
    for ct in range(4):
        D_pad[np.arange(128), ct, np.arange(128)] = dsk[ct * 128:(ct + 1) * 128]
    m["D_pad"] = D_pad.reshape(128, 512)
    m["ident"] = np.eye(128, dtype=np.float32)
    for si, L in enumerate((Lp, Ls)):
        for k, v in fft_consts(L).items():
            m["c%d_%s" % (si, k)] = v
        ft, tl = filter_pos_tables(L)
        m["c%d_feats" % si] = ft
        m["c%d_tl" % si] = tl
    return m


def build(Lp, Ls, shapes, debug=False):
    nc = bass.Bass("TRN2", target_bir_lowering=False)
    P = Prog(nc)
    Ltot = Lp + Ls
    seqs = [(0, Lp), (Lp, Ls)]
    I = {k: nc.dram_tensor(k, list(s), F32, kind="ExternalInput").ap() for k, s in shapes.items()}
    yout = nc.dram_tensor("y", [Ltot, D], F32, kind="ExternalOutput").ap()
    skind = "ExternalOutput" if debug else "Internal"
    def scratch(name, shape):
        return nc.dram_tensor(name, list(shape), F32, kind=skind).ap()
    X1T = scratch("X1T", [D, Ltot])
    US = scratch("US", [DS, Ltot])
    VS = [scratch("VS%d" % i, [DH, Ltot]) for i in range(3)]
    YS = scratch("YS", [DS, Ltot])
    YH = scratch("YH", [DH, Ltot])
    KT = [scratch("KT%d" % si, [2, DH, 2 * L]) for si, (_, L) in enumerate(seqs)]
    BBW = scratch("BBW", [2, 128, 2 * 8 * 2048])
    WSPEC = {"wg1": (D, DFF), "wu1": (D, DFF), "wd1": (DFF, D), "win": (D, DIN), "gluw": (DS, DS), "wout": (D, D),
             "wg2": (D, DFF), "wu2": (D, DFF), "wd2": (DFF, D)}
    WS = {}
    FFNW = ("wg1", "wu1", "wd1", "wg2", "wu2", "wd2", "win", "gluw", "wout")
    for wn, (kin, nout) in WSPEC.items():
        WS[wn] = nc.dram_tensor("WS_" + wn, [nout // 128, 128, kin], BF16 if wn in FFNW else F32, kind="Internal").ap()

    V, S, G_, T = "dve", "act", "pool", "pe"

    def f32(ap):
        return ap.bitcast(F32)

    def rcopy(eng, out, in_, reads, writes):
        if eng == S:
            P.op(S, "activation", out, in_, AF.Copy, reads=reads, writes=writes)
        else:
            P.op(eng, "tensor_copy", out, in_, reads=reads, writes=writes)

    cols = P.sb("cols", [128, NCOLS])
    P.dma(cols[:], I["cols"][:, :], writes=["cols"])
    ones_f = P.sb("ones_f", [128, 128])
    P.op(V, "memset", ones_f[:], 1.0, writes=["ones_f"])
    ones = P.sb("ones", [128, 128], R32)
    P.op(V, "tensor_copy", ones[:], ones_f[:], reads=["ones_f"], writes=["ones"])
    ident = P.sb("ident", [128, 128])
    P.dma(ident[:], I["ident"][:, :], writes=["ident"])
    cst = P.sb("cst", [128, 4])
    P.op(V, "memset", cst[:, 0:1], LN_EPS, writes=["cst0"])
    P.op(V, "memset", cst[:, 1:2], RMS_EPS, writes=["cst1"])
    P.op(V, "memset", cst[:, 2:3], math.pi / 2, writes=["cst2"])
    P.op(V, "memset", cst[:, 3:4], 0.0, writes=["cst3"])
    CSTK = ["cst0", "cst1", "cst2", "cst3", "cols", "ones", "ident"]

    def col(name, k):
        o, w = COLS[name]
        return cols[:, o + k:o + k + 1]

    def rsin(eng, out, in_, shift, tmp, np_, rk, wk, tk):
        e = eng
        P.op(e, "tensor_scalar", tmp, in_, float(shift), 1.0 / TWO_PI, ALU.add, ALU.mult, reads=rk, writes=tk)
        P.op(e, "tensor_scalar_add", tmp, tmp, MAGIC, reads=tk, writes=tk)
        P.op(e, "tensor_scalar_add", tmp, tmp, -MAGIC, reads=tk, writes=tk)
        P.op(e, "scalar_tensor_tensor", tmp, tmp, -TWO_PI, in_, ALU.mult, ALU.add, reads=tk + rk, writes=tk)
        if shift == 0.0:
            P.op(S, "activation", out, tmp, AF.Sin, reads=tk, writes=wk)
        else:
            P.op(S, "activation", out, tmp, AF.Sin, bias=cst[:np_, 2:3], reads=tk + ["cst2"], writes=wk)

    qi = 0
    for wn, (kin, nout) in WSPEC.items():
        if wn in FFNW:
            continue
        src = I[wn].rearrange("(kc p) (n c) -> n p kc c", p=128, c=128)
        dst = WS[wn].rearrange("n p (kc c) -> n p kc c", c=128)
        for n in range(nout // 128):
            P.dma(dst[n], src[n], writes=[("WS", wn, n)], eng=("sp", "act", "pool")[qi % 3])
            qi += 1
    def weight_convert_gen(wf, wb_):
        ci_ = 0
        for wn in FFNW:
            kin, nout = WSPEC[wn]
            nk = kin // 128
            src = I[wn].rearrange("(kc p) (n c) -> n p kc c", p=128, c=128)
            dst = WS[wn].rearrange("n p (kc c) -> n p kc c", c=128)
            for n in range(nout // 128):
                b3 = ci_ % 2; ci_ += 1
                P.dma(wf[0][:, 0:nk, :], src[n], writes=[("wcf", 0)], eng=("sp", "pool")[ci_ % 2])
                rcopy(S if ci_ % 2 else V, wb_[b3][:, 0:nk, :], wf[0][:, 0:nk, :], [("wcf", 0)], [("wcb", b3)])
                P.dma(dst[n], wb_[b3][:, 0:nk, :], reads=[("wcb", b3)], writes=[("WS", wn, n)], eng=("pool", "sp")[ci_ % 2])
                yield

    if 'F' not in _SKIP:
      with P.phase():
          wcf_ = [P.sb("wcf%d" % i, [128, FC, 128]) for i in range(1)]
          wcb_ = [P.sb("wcb%d" % i, [128, FC, 128], BF16) for i in range(2)]
          wgen = weight_convert_gen(wcf_, wcb_)
          fw1 = P.sb("fw1", [17, 64]); fw2 = P.sb("fw2", [64, 64]); fw3 = P.sb("fw3", [64, 2048])
          fcols = P.sb("fcols", [64, 6]); nrate = P.sb("nrate", [1, 2048])
          P.dma(fw1[:], I["fw1"][:, :], writes=["fw1"])
          P.dma(fw2[:], I["fw2"][:, :], writes=["fw2"])
          P.dma(fw3[:], I["fw3"][:, :], writes=["fw3"])
          P.dma(fcols[:, 0:4], I["fcols"][:, :], writes=["fcols"])
          P.dma(nrate[:], I["logdec"][:, :], writes=["nrate"])
          P.op(S, "activation", nrate[:], nrate[:], AF.Exp, reads=["nrate"], writes=["nrate"])
          P.op(V, "tensor_scalar_mul", nrate[:], nrate[:], -1.0, reads=["nrate"], writes=["nrate"])
          P.op(V, "tensor_tensor", fcols[:, 4:5], fcols[:, 0:1], fcols[:, 1:2], ALU.mult, reads=["fcols"], writes=["fc4"])
          P.op(V, "tensor_tensor", fcols[:, 5:6], fcols[:, 2:3], fcols[:, 3:4], ALU.mult, reads=["fcols"], writes=["fc5"])
          Nmax = 2 * max(Lp, Ls)
          H2 = P.sb("H2", [64, Nmax], R32)
          fw3r = P.sb("fw3r", [64, 2048], R32)
          rcopy(V, fw3r[:], fw3[:], ["fw3"], ["fw3r"])
          kbuf = P.sb("kbuf", [128, Nmax])
          ft = [P.sb("ft%d" % i, [17, 512]) for i in range(2)]
          tlb = [P.sb("tlb%d" % i, [1, 512]) for i in range(3)]
          fa_ = [P.sb("fa%d" % i, [64, 512]) for i in range(2)]
          ftmp = [P.sb("ftmp%d" % i, [64, 512]) for i in range(2)]
          fh1 = [P.sb("fh1%d" % i, [64, 512]) for i in range(2)]
          dec = [P.sb("dec%d" % i, [128, 512]) for i in range(2)]
          absb = [P.sb("absb%d" % i, [128, 512]) for i in range(2)]
          asum = P.sb("asum", [128, 40])
          psF = [P.ps("psF%d" % i, [128, 512]) for i in range(4)]
          it = 0
          for si, (off, L) in enumerate(seqs):
              N = 2 * L
              ncol = N // 512
              feats = I["c%d_feats" % si]; tl = I["c%d_tl" % si]
              for j in range(ncol):
                  next(wgen, None)
                  b = it % 2; it += 1
                  cs = slice(j * 512, (j + 1) * 512)
                  P.dma(ft[b][:], feats[:, cs], writes=[("ft", b)])
                  pa = psF[b]
                  P.op(T, "matmul", pa[:64, :], fw1[:, :], ft[b][:], start=True, stop=True,
                       reads=["fw1", ("ft", b)], writes=[("psF", b)])
                  P.op(V, "tensor_scalar", fa_[b][:], pa[:64, :], fcols[:, 0:1], fcols[:, 4:5], ALU.mult, ALU.add,
                       reads=[("psF", b), "fcols", "fc4"], writes=[("fa", b)])
                  rsin(V, fh1[b][:], fa_[b][:], 0.0, ftmp[b][:], 64, [("fa", b)], [("fh1", b)], [("ftmp", b)])
                  pb = psF[2 + b]
                  P.op(T, "matmul", pb[:64, :], fw2[:, :], fh1[b][:], start=True, stop=True,
                       reads=["fw2", ("fh1", b)], writes=[("psF", 2 + b)])
                  P.op(V, "tensor_scalar", fa_[b][:], pb[:64, :], fcols[:, 2:3], fcols[:, 5:6], ALU.mult, ALU.add,
                       reads=[("psF", 2 + b), "fcols", "fc5"], writes=[("fa", b)])
                  rsin(V, H2[:, cs], fa_[b][:], 0.0, ftmp[b][:], 64, [("fa", b)], [("H2", j)], [("ftmp", b)])
              for o in range(2):
                  for cc in range(4):
                      for j in range(ncol):
                          if j % 2 == 0:
                              next(wgen, None)
                          b = it % 2; b3 = it % 3; it += 1
                          cs = slice(j * 512, (j + 1) * 512)
                          dr = 0 if (j * 512) < L else 1
                          n0 = o * 1024 + dr * 512 + cc * 128
                          P.dma(tlb[b3][:], tl[:, cs], writes=[("tlb", b3)])
                          pe_ = psF[b]
                          P.op(T, "matmul", pe_[:, :], nrate[0:1, n0:n0 + 128], tlb[b3][:], start=True, stop=True,
                               reads=["nrate", ("tlb", b3)], writes=[("psF", b)])
                          P.op(S, "activation", dec[b][:], pe_[:, :], AF.Exp, reads=[("psF", b)], writes=[("dec", b)])
                          pk = psF[2 + b]
                          P.op(T, "matmul", pk[:, :], fw3r[:, n0:n0 + 128], H2[:, cs], start=True, stop=True,
                               reads=["fw3r", ("H2", j)], writes=[("psF", 2 + b)])
                          P.op(V, "tensor_tensor", kbuf[:, cs], pk[:, :], dec[b][:], ALU.mult,
                               reads=[("psF", 2 + b), ("dec", b)], writes=[("kbuf", j)])
                          if j * 512 == L:
                              P.op(V, "memset", kbuf[:, L:L + 1], 0.0, reads=[("kbuf", j)], writes=[("kbuf", j)])
                          P.op(S, "activation", absb[b][:], kbuf[:, cs], AF.Abs,
                               reads=[("kbuf", j)], writes=[("absb", b)])
                          P.op(V, "reduce_sum", asum[:, j:j + 1], absb[b][:], AX.X, reads=[("absb", b)], writes=[("asum", j)])
                      ak = [("asum", j) for j in range(ncol)]
                      P.op(V, "reduce_sum", asum[:, 32:33], asum[:, 0:ncol], AX.X, reads=ak, writes=["asumT"])
                      P.op(V, "tensor_scalar_add", asum[:, 33:34], asum[:, 32:33], FILTER_EPS, reads=["asumT"], writes=["asumE"])
                      P.op(V, "reciprocal", asum[:, 34:35], asum[:, 33:34], reads=["asumE"], writes=["asumR"])
                      kk = [("kbuf", j) for j in range(ncol)]
                      P.op(V, "tensor_scalar_mul", kbuf[:, 0:N], kbuf[:, 0:N], asum[:, 34:35], reads=kk + ["asumR"], writes=kk)
                      P.dma(KT[si][o, cc * 128:(cc + 1) * 128, :], kbuf[:, 0:N], reads=kk, writes=[("KT", si, o, cc)])
          for _ in wgen:
              pass

    NPW = 14
    sp_ = P.sb("s5par", [128, 4 * 32 + 2 * NPW * 32 + 2 * 9 * 32])
    MAGs = sp_[:, 0:32]; URs = sp_[:, 32:64]; UIs = sp_[:, 64:96]; MAG8 = sp_[:, 96:128]
    def PWR(s): return sp_[:, 128 + 32 * s:128 + 32 * s + 32]
    def PWI(s): return sp_[:, 128 + 32 * NPW + 32 * s:128 + 32 * NPW + 32 * s + 32]
    _lp0 = 128 + 2 * 32 * NPW
    def LPR(e): return sp_[:, _lp0 + 32 * e:_lp0 + 32 * e + 32]
    def LPI(e): return sp_[:, _lp0 + 288 + 32 * e:_lp0 + 288 + 32 * e + 32]

    def discretize(pfx, lre, lim, lst, Fw, mag, cr, ci, tmp):
        k = lambda n: [pfx + n]
        P.op(S, "activation", lst, lst, AF.Exp, reads=k("lst"), writes=k("lst"))
        P.op(V, "tensor_tensor", mag, lre, lst, ALU.mult, reads=k("lre") + k("lst"), writes=k("mag"))
        P.op(S, "activation", mag, mag, AF.Exp, reads=k("mag"), writes=k("mag"))
        P.op(V, "tensor_tensor", lst, lim, lst, ALU.mult, reads=k("lim") + k("lst"), writes=k("lst"))
        rsin(V, ci, lst, 0.0, tmp, 128, k("lst"), k("ci"), k("tmp"))
        rsin(V, cr, lst, math.pi / 2, tmp, 128, k("lst"), k("cr"), k("tmp"))

    if 'S' not in _SKIP:
      with P.phase():
          ls = [P.sb("ls%d" % i, [128, 32]) for i in range(3)]
          ltmp = P.sb("ltmp", [128, 32])
          for i, n in enumerate(("lre", "lim", "lst")):
              P.dma(ls[i][:], I["lam_s"][i], writes=["s_" + n])
          discretize("s_", ls[0][:], ls[1][:], ls[2][:], 32, MAGs, URs, UIs, ltmp[:])
          P.op(V, "tensor_copy", PWR(0), URs, reads=["s_cr"], writes=[("pw", 0)])
          P.op(V, "tensor_copy", PWI(0), UIs, reads=["s_ci"], writes=[("pw", 0)])
          pt = [P.sb("pt%d" % i, [128, 32]) for i in range(2)]
          for s in range(NPW - 1):
              P.op(V, "tensor_tensor", pt[0][:], PWR(s), PWR(s), ALU.mult, reads=[("pw", s)], writes=["pt0"])
              P.op(V, "tensor_tensor", pt[1][:], PWI(s), PWI(s), ALU.mult, reads=[("pw", s)], writes=["pt1"])
              P.op(V, "tensor_tensor", PWR(s + 1), pt[0][:], pt[1][:], ALU.subtract, reads=["pt0", "pt1"], writes=[("pw", s + 1)])
              P.op(V, "tensor_tensor", pt[0][:], PWR(s), PWI(s), ALU.mult, reads=[("pw", s)], writes=["pt0"])
              P.op(V, "tensor_scalar_mul", PWI(s + 1), pt[0][:], 2.0, reads=["pt0"], writes=[("pw", s + 1)])
          P.op(V, "memset", LPR(0), 1.0, writes=[("lp", 0)])
          P.op(V, "memset", LPI(0), 0.0, reads=[("lp", 0)], writes=[("lp", 0)])
          P.op(V, "tensor_tensor", LPR(1), MAGs, URs, ALU.mult, reads=["s_mag", "s_cr"], writes=[("lp", 1)])
          P.op(V, "tensor_tensor", LPI(1), MAGs, UIs, ALU.mult, reads=["s_mag", "s_ci", ("lp", 1)], writes=[("lp", 1)])
          for e in range(1, 8):
              P.op(V, "tensor_tensor", pt[0][:], LPR(e), LPR(1), ALU.mult, reads=[("lp", e), ("lp", 1)], writes=["pt0"])
              P.op(V, "tensor_tensor", pt[1][:], LPI(e), LPI(1), ALU.mult, reads=[("lp", e), ("lp", 1)], writes=["pt1"])
              P.op(V, "tensor_tensor", LPR(e + 1), pt[0][:], pt[1][:], ALU.subtract, reads=["pt0", "pt1"], writes=[("lp", e + 1)])
              P.op(V, "tensor_tensor", pt[0][:], LPR(e), LPI(1), ALU.mult, reads=[("lp", e), ("lp", 1)], writes=["pt0"])
              P.op(V, "tensor_tensor", pt[1][:], LPI(e), LPR(1), ALU.mult, reads=[("lp", e), ("lp", 1)], writes=["pt1"])
              P.op(V, "tensor_tensor", LPI(e + 1), pt[0][:], pt[1][:], ALU.add, reads=["pt0", "pt1", ("lp", e + 1)], writes=[("lp", e + 1)])
          P.op(V, "tensor_tensor", pt[0][:], MAGs, MAGs, ALU.mult, reads=["s_mag"], writes=["pt0"])
          P.op(V, "tensor_tensor", pt[1][:], pt[0][:], pt[0][:], ALU.mult, reads=["pt0"], writes=["pt1"])
          P.op(V, "tensor_tensor", MAG8, pt[1][:], pt[1][:], ALU.mult, reads=["pt1"], writes=["mag8"])
          A = {n: P.sb("pd_%s" % n, [128, 2048]) for n in
               ("lre", "lim", "lst", "mag", "cr", "ci", "tmp", "bre", "bim", "qr", "qi", "t1", "t2")}
          for d in range(2):
              ds_ = slice(d * 2048, (d + 1) * 2048)
              pk = lambda n: ["p_%s" % n]
              for i, n in enumerate(("lre", "lim", "lst")):
                  P.dma(A[n][:], I["lam_pad"][i][:, ds_], writes=pk(n))
              P.dma(A["bre"][:], I["B_pad"][0][:, ds_], writes=pk("bre"))
              P.dma(A["bim"][:], I["B_pad"][1][:, ds_], writes=pk("bim"))
              discretize("p_", A["lre"][:], A["lim"][:], A["lst"][:], 2048, A["mag"][:], A["cr"][:], A["ci"][:], A["tmp"][:])
              P.op(V, "tensor_tensor", A["cr"][:], A["cr"][:], A["mag"][:], ALU.mult, reads=pk("cr") + pk("mag"), writes=pk("cr"))
              P.op(V, "tensor_scalar_add", A["cr"][:], A["cr"][:], -1.0, reads=pk("cr"), writes=pk("cr"))
              P.op(V, "tensor_tensor", A["ci"][:], A["ci"][:], A["mag"][:], ALU.mult, reads=pk("ci") + pk("mag"), writes=pk("ci"))
              P.op(V, "tensor_tensor", A["t1"][:], A["lre"][:], A["lre"][:], ALU.mult, reads=pk("lre"), writes=pk("t1"))
              P.op(V, "tensor_tensor", A["t2"][:], A["lim"][:], A["lim"][:], ALU.mult, reads=pk("lim"), writes=pk("t2"))
              P.op(V, "tensor_tensor", A["mag"][:], A["t1"][:], A["t2"][:], ALU.add, reads=pk("t1") + pk("t2") + pk("mag"), writes=pk("mag"))
              P.op(V, "reciprocal", A["mag"][:], A["mag"][:], reads=pk("mag"), writes=pk("mag"))
              P.op(V, "tensor_tensor", A["t1"][:], A["cr"][:], A["lre"][:], ALU.mult, reads=pk("cr") + pk("lre"), writes=pk("t1"))
              P.op(V, "tensor_tensor", A["t2"][:], A["ci"][:], A["lim"][:], ALU.mult, reads=pk("ci") + pk("lim"), writes=pk("t2"))
              P.op(V, "tensor_tensor", A["qr"][:], A["t1"][:], A["t2"][:], ALU.add, reads=pk("t1") + pk("t2"), writes=pk("qr"))
              P.op(V, "tensor_tensor", A["qr"][:], A["qr"][:], A["mag"][:], ALU.mult, reads=pk("qr") + pk("mag"), writes=pk("qr"))
              P.op(V, "tensor_tensor", A["t1"][:], A["ci"][:], A["lre"][:], ALU.mult, reads=pk("ci") + pk("lre"), writes=pk("t1"))
              P.op(V, "tensor_tensor", A["t2"][:], A["cr"][:], A["lim"][:], ALU.mult, reads=pk("cr") + pk("lim"), writes=pk("t2"))
              P.op(V, "tensor_tensor", A["qi"][:], A["t1"][:], A["t2"][:], ALU.subtract, reads=pk("t1") + pk("t2"), writes=pk("qi"))
              P.op(V, "tensor_tensor", A["qi"][:], A["qi"][:], A["mag"][:], ALU.mult, reads=pk("qi") + pk("mag"), writes=pk("qi"))
              P.op(V, "tensor_tensor", A["t1"][:], A["qr"][:], A["bre"][:], ALU.mult, reads=pk("qr") + pk("bre"), writes=pk("t1"))
              P.op(V, "tensor_tensor", A["t2"][:], A["qi"][:], A["bim"][:], ALU.mult, reads=pk("qi") + pk("bim"), writes=pk("t2"))
              P.op(V, "tensor_tensor", A["lre"][:], A["t1"][:], A["t2"][:], ALU.subtract, reads=pk("t1") + pk("t2") + pk("lre"), writes=pk("lre"))
              P.op(V, "tensor_tensor", A["t1"][:], A["qr"][:], A["bim"][:], ALU.mult, reads=pk("qr") + pk("bim"), writes=pk("t1"))
              P.op(V, "tensor_tensor", A["t2"][:], A["qi"][:], A["bre"][:], ALU.mult, reads=pk("qi") + pk("bre"), writes=pk("t2"))
              P.op(V, "tensor_tensor", A["lim"][:], A["t1"][:], A["t2"][:], ALU.add, reads=pk("t1") + pk("t2") + pk("lim"), writes=pk("lim"))
              P.op(V, "tensor_scalar_add", A["cr"][:], A["cr"][:], 1.0, reads=pk("cr"), writes=pk("cr"))
              BWv = [BBW[r].rearrange("q (d e x) -> q d e x", d=2, e=8) for r in range(2)]
              cur = ("lre", "lim"); nxt = ("qr", "qi")
              for e in range(8):
                  P.dma(BWv[0][:, d, e, :], A[cur[0]][:], reads=pk(cur[0]), writes=[("BBW", 0, d, e)])
                  P.dma(BWv[1][:, d, e, :], A[cur[1]][:], reads=pk(cur[1]), writes=[("BBW", 1, d, e)])
                  if e == 7:
                      break
                  P.op(V, "tensor_tensor", A["t1"][:], A[cur[0]][:], A["cr"][:], ALU.mult, reads=pk(cur[0]) + pk("cr"), writes=pk("t1"))
                  P.op(G_, "tensor_tensor", A["t2"][:], A[cur[1]][:], A["ci"][:], ALU.mult, reads=pk(cur[1]) + pk("ci"), writes=pk("t2"))
                  P.op(V, "tensor_tensor", A[nxt[0]][:], A["t1"][:], A["t2"][:], ALU.subtract, reads=pk("t1") + pk("t2") + pk(nxt[0]), writes=pk(nxt[0]))
                  P.op(V, "tensor_tensor", A["t1"][:], A[cur[0]][:], A["ci"][:], ALU.mult, reads=pk(cur[0]) + pk("ci"), writes=pk("t1"))
                  P.op(G_, "tensor_tensor", A["t2"][:], A[cur[1]][:], A["cr"][:], ALU.mult, reads=pk(cur[1]) + pk("cr"), writes=pk("t2"))
                  P.op(V, "tensor_tensor", A[nxt[1]][:], A["t1"][:], A["t2"][:], ALU.add, reads=pk("t1") + pk("t2") + pk(nxt[1]), writes=pk(nxt[1]))
                  cur, nxt = nxt, cur

    def ln_inplace(xT, xk, gname, bname, tl, ps_s, ps_q, want_r=True, want_b=False):
        xTr = tl["xTr"]
        for k in range(KC):
            sq = tl["sq"][k % 2]
            P.op(S, "activation", sq[:], xT[:, k, :], AF.Square, reads=[(xk, k)], writes=[("sq", k % 2)])
            rcopy(V if k % 2 else S, xTr[:, k, :], xT[:, k, :], [(xk, k)], [("xTr", k)])
            P.op(T, "matmul", ps_s[:, :], ones[:, :], xTr[:, k, :], start=(k == 0), stop=(k == KC - 1),
                 reads=["ones", ("xTr", k)], writes=["ps_s"])
            P.op(T, "matmul", ps_q[:, :], ones[:, :], sq[:], start=(k == 0), stop=(k == KC - 1),
                 reads=["ones", ("sq", k % 2)], writes=["ps_q"])
        mean, var = tl["mean"], tl["var"]
        P.op(S, "activation", mean[:], ps_s[:, :], AF.Copy, scale=1.0 / D, reads=["ps_s"], writes=["mean"])
        P.op(V, "tensor_tensor", var[:], mean[:], mean[:], ALU.mult, reads=["mean"], writes=["var"])
        P.op(V, "scalar_tensor_tensor", var[:], ps_q[:, :], 1.0 / D, var[:], ALU.mult, ALU.subtract,
             reads=["ps_q", "var"], writes=["var"])
        P.op(S, "activation", var[:], var[:], AF.Sqrt, bias=cst[:, 0:1], reads=["var", "cst0"], writes=["var"])
        P.op(V, "reciprocal", var[:], var[:], reads=["var"], writes=["var"])
        for k in range(KC):
            e = V if k % 2 == 0 else G_
            P.op(e, "tensor_tensor", xT[:, k, :], xT[:, k, :], mean[:], ALU.subtract, reads=[(xk, k), "mean"], writes=[(xk, k)])
            P.op(e, "tensor_tensor", xT[:, k, :], xT[:, k, :], var[:], ALU.mult, reads=[(xk, k), "var"], writes=[(xk, k)])
            P.op(V, "tensor_scalar", xT[:, k, :], xT[:, k, :], col(gname, k), col(bname, k), ALU.mult, ALU.add,
                 reads=[(xk, k), "cols"], writes=[(xk, k)])
            if want_r:
                rcopy(S, xTr[:, k, :], xT[:, k, :], [(xk, k)], [("xTr", k)])
            if want_b:
                rcopy(G_ if k % 2 else S, tl["xTb"][:, k, :], xT[:, k, :], [(xk, k)], [("xTb", k)])

    class WStream:
        CAP = {"s": 3, "b": 8, "d": 4}

        def __init__(self, tl, seq):
            self.tl = tl; self.seq = seq; self.pos = 0
            self.issued_n = {k: 0 for k in self.CAP}; self.ret_n = {k: 0 for k in self.CAP}
            self.done = [False] * len(seq); self.h = {}
            self.cnt = {"st": 0, "g": 0, "d": 0, "e": 0, "b": 0}

        def _issue(self, i):
            tl = self.tl
            wn, n, nk, kind = self.seq[i]
            c = self.cnt
            view = WS[wn].rearrange("n p (kc c) -> n p kc c", c=128)[n]
            if kind == "s":
                st = c["st"] % 2; c["st"] += 1
                sg = c["g"] % 3; c["g"] += 1
                P.dma(tl["wst"][st][:, 0:nk, :], view, reads=[("WS", wn, n)], writes=[("wst", st)])
                e = (S, V)[c["e"] % 2]; c["e"] += 1
                rcopy(e, tl["wr"][sg][:, 0:nk, :], tl["wst"][st][:, 0:nk, :], [("wst", st)], [("wr", sg)])
                self.h[i] = (tl["wr"][sg], ("wr", sg))
            elif kind == "b":
                sb_ = c["b"] % 8; c["b"] += 1
                P.dma(tl["wb"][sb_][:, 0:nk, :], view, reads=[("WS", wn, n)], writes=[("wb", sb_)])
                self.h[i] = (tl["wb"][sb_], ("wb", sb_))
            else:
                sd = c["d"] % 4; c["d"] += 1
                P.dma(tl["wdb"][sd][:, 0:nk, :], view, reads=[("WS", wn, n)], writes=[("wdb", sd)])
                self.h[i] = (tl["wdb"][sd], ("wdb", sd))
            self.issued_n[kind] += 1
            self.done[i] = True

        def get(self):
            i = self.pos; self.pos += 1
            blocked = set()
            for j in range(i, min(len(self.seq), i + 12)):
                kind = self.seq[j][3]
                if self.done[j] or kind in blocked:
                    continue
                if self.issued_n[kind] - self.ret_n[kind] < self.CAP[kind]:
                    self._issue(j)
                else:
                    blocked.add(kind)
            assert self.done[i]
            self.ret_n[self.seq[i][3]] += 1
            return self.h.pop(i)

    def ffn_seq(wg, wu, wd):
        q = []
        for f in range(FC):
            q.append((wg, f, KC, "b")); q.append((wu, f, KC, "b"))
        for dch in range(KC):
            q.append((wd, dch, FC, "d"))
        return q

    def ffn_inplace(xT, xk, ws_, tl, psG, psU, psD):
        hT = tl["hT"]; xTb = tl["xTb"]
        for f in range(FC):
            b = f % 2
            wgt, kg_ = ws_.get()
            for k in range(KC):
                P.op(T, "matmul", psG[b][:, :], wgt[:, k, :], xTb[:, k, :], start=(k == 0), stop=(k == KC - 1),
                     reads=[kg_, ("xTb", k)], writes=[("psG", b)])
            wut, ku_ = ws_.get()
            for k in range(KC):
                P.op(T, "matmul", psU[b][:, :], wut[:, k, :], xTb[:, k, :], start=(k == 0), stop=(k == KC - 1),
                     reads=[ku_, ("xTb", k)], writes=[("psU", b)])
            P.op(S, "activation", tl["s"][b][:], psG[b][:, :], AF.Silu, reads=[("psG", b)], writes=[("s", b)])
            P.op(V, "tensor_tensor", hT[:, f, :], tl["s"][b][:], psU[b][:, :], ALU.mult,
                 reads=[("s", b), ("psU", b)], writes=[("hT", f)])
        for dch in range(KC):
            b = dch % 2
            wdt, kd_ = ws_.get()
            for f in range(FC):
                P.op(T, "matmul", psD[b][:, :], wdt[:, f, :], hT[:, f, :], start=(f == 0), stop=(f == FC - 1),
                     reads=[kd_, ("hT", f)], writes=[("psD", b)])
            P.op(S, "activation", tl["s"][b][:], psD[b][:, :], AF.Copy, scale=0.5, reads=[("psD", b)], writes=[("s", b)])
            P.op(V, "scalar_tensor_tensor", xT[:, dch, :], xT[:, dch, :], ALPHA, tl["s"][b][:], ALU.mult, ALU.add,
                 reads=[(xk, dch), ("s", b)], writes=[(xk, dch)])

    def alloc_row_tiles():
        tl = {}
        tl["xT"] = P.sb("xT", [128, KC, TT])
        tl["xTr"] = P.sb("xTr", [128, KC, TT], R32)
        tl["xTb"] = P.sb("xTb", [128, KC, TT], BF16)
        tl["hT"] = P.sb("hT", [128, FC, TT], BF16)
        tl["gm"] = P.sb("gm", [128, 12, TT], BF16)
        tl["wb"] = [P.sb("wb%d" % i, [128, KC, 128], BF16) for i in range(8)]
        tl["wdb"] = [P.sb("wdb%d" % i, [128, FC, 128], BF16) for i in range(4)]
        tl["s"] = [P.sb("s%d" % i, [128, TT]) for i in range(2)]
        tl["sq"] = [P.sb("sq%d" % i, [128, TT], R32) for i in range(2)]
        tl["mean"] = P.sb("mean", [128, TT]); tl["var"] = P.sb("var", [128, TT])
        tl["stg"] = P.sb("stg", [128, 10, TT])
        psG = [P.ps("psG%d" % i, [128, TT]) for i in range(2)]
        psU = [P.ps("psU%d" % i, [128, TT]) for i in range(2)]
        psD = [P.ps("psD%d" % i, [128, TT]) for i in range(2)]
        ps_s = P.ps("ps_s", [128, TT]); ps_q = P.ps("ps_q", [128, TT])
        return tl, psG, psU, psD, ps_s, ps_q

    X1Tv = X1T.rearrange("(k p) t -> p k t", p=128)

    if 'A' not in _SKIP:
      with P.phase():
          tl, psG, psU, psD, ps_s, ps_q = alloc_row_tiles()
          xT = tl["xT"]; xTr = tl["xTr"]; hT = tl["hT"]; stg = tl["stg"]
          halo = P.sb("halo", [128, 12, 2])
          PB = [P.sb("PB%d" % i, [128, 515]) for i in range(2)]
          CO = [P.sb("CO%d" % i, [128, 515]) for i in range(2)]
          for i in range(2):
              P.op(V, "memset", PB[i][:], 0.0, writes=[("PB", i)])
          winv = I["win"].rearrange("(kc p) (n c) -> n p kc c", p=128, c=128)
          wseq = []
          for ti in range(Ltot // TT):
              wseq += ffn_seq("wg1", "wu1", "wd1") + [("win", c, KC, "b") for c in range(16)]
          ws_ = WStream(tl, wseq)
          USv = US.rearrange("(k p) t -> p k t", p=128)
          xin = P.sb("xin", [128, 8, TT])
          def load_x(tj):
              for sub in range(4):
                  P.dma(xin[:, 2 * sub:2 * sub + 2, :], I["x"][tj * TT + sub * 128:tj * TT + (sub + 1) * 128, :].rearrange("t (a b) -> t a b", b=TT),
                        writes=[("xin", 2 * sub), ("xin", 2 * sub + 1)])
          for ti in range(Ltot // TT):
              t0 = ti * TT
              first = any(t0 == off for off, L in seqs)
              last = any(t0 + TT == off + L for off, L in seqs)
              if ti == 0:
                  load_x(0)
              for k in range(KC):
                  b = k % 2
                  for sub in range(4):
                      P.op(T, "transpose", psD[b][:, sub * 128:(sub + 1) * 128],
                           xin[:, 2 * sub + k // 4, (k % 4) * 128:(k % 4 + 1) * 128], ident[:, :],
                           reads=[("xin", 2 * sub + k // 4), "ident"], writes=[("psD", b)])
                  rcopy(S if k % 2 else V, xT[:, k, :], psD[b][:, :], [("psD", b)], [("xT", k)])
                  rcopy(V if k % 2 else S, tl["xTb"][:, k, :], xT[:, k, :], [("xT", k)], [("xTb", k)])
              if ti + 1 < Ltot // TT:
                  load_x(ti + 1)
              ffn_inplace(xT, "xT", ws_, tl, psG, psU, psD)
              ln_inplace(xT, "xT", "ln1g", "ln1b", tl, ps_s, ps_q, want_r=False, want_b=True)
              P.dma(X1Tv[:, :, t0:t0 + TT], xT[:, :, :], reads=[("xT", k) for k in range(KC)], writes=[("X1T", ti)])
              for c in range(16):
                  b = c % 2
                  wt_, kw_ = ws_.get()
                  for k in range(KC):
                      P.op(T, "matmul", psD[b][:, :], wt_[:, k, :], tl["xTb"][:, k, :], start=(k == 0), stop=(k == KC - 1),
                           reads=[kw_, ("xTb", k)], writes=[("psD", b)])
                  if c < 4:
                      P.op(S, "activation", stg[:, c, :], psD[b][:, :], AF.Copy, reads=[("psD", b)], writes=[("stg", c)])
                      if c == 3:
                          P.dma(USv[:, :, t0:t0 + TT], stg[:, 0:4, :], reads=[("stg", q) for q in range(4)], writes=[("US", ti)])
                      continue
                  hc = c - 4
                  pb = PB[hc % 2]; co = CO[hc % 2]
                  kb = ("PB", hc % 2); kc_ = ("CO", hc % 2)
                  P.op(S, "activation", pb[:, 2:514], psD[b][:, :], AF.Copy, reads=[("psD", b)], writes=[kb])
                  if first:
                      P.op(G_, "memset", pb[:, 0:2], 0.0, reads=[kb], writes=[kb])
                  else:
                      P.op(G_, "tensor_copy", pb[:, 0:2], halo[:, hc, :], reads=[kb, ("halo", hc)], writes=[kb])
                  lo = 2 if first else 1
                  hi = 513 if last else 512
                  n_ = hi - lo + 1
                  P.op(V, "tensor_scalar", co[:, lo:lo + n_], pb[:, lo - 1:lo - 1 + n_], col("sw0", hc), col("sb", hc),
                       ALU.mult, ALU.add, reads=[kb, "cols"], writes=[kc_])
                  P.op(V, "scalar_tensor_tensor", co[:, lo:lo + n_], pb[:, lo:lo + n_], col("sw1", hc), co[:, lo:lo + n_],
                       ALU.mult, ALU.add, reads=[kb, kc_, "cols"], writes=[kc_])
                  P.op(V, "scalar_tensor_tensor", co[:, lo:lo + n_], pb[:, lo + 1:lo + 1 + n_], col("sw2", hc), co[:, lo:lo + n_],
                       ALU.mult, ALU.add, reads=[kb, kc_, "cols"], writes=[kc_])
                  P.op(G_, "tensor_copy", halo[:, hc, :], pb[:, 512:514], reads=[kb], writes=[("halo", hc)])
                  dst = VS[hc // 4]
                  r0 = (hc % 4) * 128
                  P.dma(dst[r0:r0 + 128, t0 + lo - 2:t0 + lo - 2 + n_], co[:, lo:lo + n_], reads=[kc_], writes=[("VS", hc, ti)])

    W = TT
    TAU = 8
    if 'B1' not in _SKIP:
      with P.phase():
        UTf = [P.sb("UTf%d" % i, [128, Ltot // 8]) for i in range(2)]
        UT = P.sb("UT", [128, TAU, Ltot // TAU], R32)
        yacc = P.sb("yacc", [128, Ltot])
        wbf = P.sb("wbf", [128, 2, 8, 128]); wbr = P.sb("wbr", [128, 2, 8, 128], R32)
        cpf = P.sb("cpf", [128, 2, 2, 128])
        dpf = P.sb("dpf", [128, 128]); dpd = P.sb("dpd", [128, 128], R32)
        VC = P.sb("VC", [128, 2, 9, 2, 128], R32)
        MI = P.sb("MI", [128, 2, 8, 128], R32)
        bT = P.sb("bT", [128, 2, 128], R32)
        vt = [P.sb("vt%d" % i, [128, 128]) for i in range(2)]
        Tr = [P.sb("Tr%d" % d, [128, W]) for d in range(2)]
        Ti = [P.sb("Ti%d" % d, [128, W]) for d in range(2)]
        NB = 2
        wk = {n: [P.sb("w_%s%d" % (n, i), [128, W]) for i in range(NB)] for n in
              ("t1", "t2", "t3", "t4", "Mr", "Mi", "Gr", "Gi")}
        cry = P.sb("cry", [128, 8])
        nU8r = P.sb("nU8r", [128, 32])
        P.op(V, "tensor_scalar_mul", nU8r[:], PWR(3), -1.0, writes=["nU8r"])
        kbase = []
        kb_ = 0
        for off, L in seqs:
            kbase.append(kb_)
            kb_ += L // TAU + 1
        HB = [[P.sb("HB%d%d" % (d, r), [128, kb_], R32) for r in range(2)] for d in range(2)]
        hz = P.sb("hz", [128, 4])
        P.op(V, "memset", hz[:], 0.0, writes=["hz"])
        for d in range(2):
            for r in range(2):
                for si, (off, L) in enumerate(seqs):
                    zc = kbase[si] if d == 0 else kbase[si] + L // TAU
                    P.op(V, "tensor_copy", HB[d][r][:, zc:zc + 1], hz[:, 0:1], reads=["hz"], writes=[("HBz", d, r, si)])
        psS = [P.ps("psS%d" % i, [128, W]) for i in range(2)]
        psO = P.ps("psO", [128, 4 * W])
        psM = [P.ps("psM%d" % i, [128, 128]) for i in range(2)]
        BWl = [BBW[r].rearrange("q (d e g s) -> q d e g s", d=2, e=8, g=16) for r in range(2)]
        CPv = [I["C_pad"][r].rearrange("s (d g q) -> s d g q", d=2, g=16) for r in range(2)]
        itn = 0
        mcnt = 0
        for ct in range(4):
            PQ = Ltot // 8
            for q4 in range(8):
                uf = UTf[q4 % 2]
                P.dma(uf[:, :], US[ct * 128:(ct + 1) * 128, q4 * PQ:(q4 + 1) * PQ], writes=[("UTf", q4 % 2)])
                for s8 in range(TAU):
                    rcopy((S, G_, V)[s8 % 3], UT[:, s8, q4 * PQ // TAU:(q4 + 1) * PQ // TAU], uf[:, s8:PQ - TAU + s8 + 1:TAU],
                          [("UTf", q4 % 2)], ["UT"])
            P.dma(dpf[:, :], I["D_pad"][:, ct * 128:(ct + 1) * 128], writes=["dpf"])
            rcopy(S, dpd[:, :], dpf[:, :], ["dpf"], ["dpd"])
            for G4 in range(4):
                G = ct * 4 + G4
                for r in range(2):
                    P.dma(cpf[:, r, :, :], CPv[r][:, :, G, :], writes=[("cpf", r)])
                for d in range(2):
                    dG = d * 16 + G
                    kwb = "wbr"
                    for r in range(2):
                        P.dma(wbf[:, r, :, :], BWl[r][:, d, :, G, :], writes=[("wbf", r)])
                        rcopy(S, wbr[:, r, :, :], wbf[:, r, :, :], [("wbf", r)], [kwb])
                    kvc = ("VC", d)
                    for e in range(9):
                        lr = LPR(e)[:, dG:dG + 1]; li = LPI(e)[:, dG:dG + 1]
                        P.op(V, "tensor_scalar_mul", vt[0][:], cpf[:, 1, d, :], li, reads=[("cpf", 1)], writes=["vt0"])
                        P.op(V, "scalar_tensor_tensor", VC[:, d, e, 0, :], cpf[:, 0, d, :], lr, vt[0][:], ALU.mult, ALU.subtract,
                             reads=[("cpf", 0), "vt0"], writes=[kvc])
                        P.op(V, "tensor_scalar_mul", vt[1][:], cpf[:, 1, d, :], lr, reads=[("cpf", 1)], writes=["vt1"])
                        P.op(V, "scalar_tensor_tensor", VC[:, d, e, 1, :], cpf[:, 0, d, :], li, vt[1][:], ALU.mult, ALU.add,
                             reads=[("cpf", 0), "vt1", kvc], writes=[kvc])
                    for r in range(2):
                        pm = psM[mcnt % 2]; kpm = ("psM", mcnt % 2); mcnt += 1
                        P.op(T, "transpose", pm[:, :], wbf[:, r, 0, :], ident[:, :], reads=[("wbf", r), "ident"], writes=[kpm])
                        P.op(S, "activation", bT[:, r, :], pm[:, :], AF.Copy, scale=(1.0 if r == 0 else -1.0), reads=[kpm], writes=[("bT", r)])
                    for lag in range(8):
                        pm = psM[mcnt % 2]; kpm = ("psM", mcnt % 2); mcnt += 1
                        P.op(T, "matmul", pm[:, :], bT[:, 0, :], VC[:, d, lag, 0, :], start=True, stop=False,
                             reads=[("bT", 0), kvc], writes=[kpm])
                        P.op(T, "matmul", pm[:, :], bT[:, 1, :], VC[:, d, lag, 1, :], start=False, stop=True,
                             reads=[("bT", 1), kvc], writes=[kpm])
                        rcopy(S if lag % 2 else V, MI[:, d, lag, :], pm[:, :], [kpm], [("MI", d)])
                    tk = ("T", d)
                    P.op(V, "memset", Tr[d][:, 0:1], 1.0, writes=[tk])
                    P.op(V, "memset", Ti[d][:, 0:1], 0.0, reads=[tk], writes=[tk])
                    n = 1; s_ = 3
                    while n < W:
                        pr = PWR(s_)[:, dG:dG + 1]; pi_ = PWI(s_)[:, dG:dG + 1]
                        t1 = wk["t1"][0]; t2 = wk["t2"][0]
                        P.op(V, "tensor_scalar_mul", t1[:, 0:n], Ti[d][:, 0:n], pi_, reads=[tk], writes=[("t1", 0)])
                        P.op(V, "tensor_scalar_mul", t2[:, 0:n], Tr[d][:, 0:n], pi_, reads=[tk], writes=[("t2", 0)])
                        P.op(V, "scalar_tensor_tensor", Tr[d][:, n:2 * n], Tr[d][:, 0:n], pr, t1[:, 0:n], ALU.mult, ALU.subtract,
                             reads=[tk, ("t1", 0)], writes=[tk])
                        P.op(V, "scalar_tensor_tensor", Ti[d][:, n:2 * n], Ti[d][:, 0:n], pr, t2[:, 0:n], ALU.mult, ALU.add,
                             reads=[tk, ("t2", 0)], writes=[tk])
                        n *= 2; s_ += 1
                    for si, (off, L) in enumerate(seqs):
                        Kc = L // TAU
                        Wk = min(W, Kc)
                        nw = Kc // Wk
                        for wi in range(nw):
                            w_ = wi if d == 0 else nw - 1 - wi
                            k0 = w_ * Wk
                            b = itn % NB; itn += 1
                            pr_, pi2 = psS[0], psS[1]
                            kpr, kpi = ("psS", 0), ("psS", 1)
                            for r, (pp, kk) in enumerate(((pr_, kpr), (pi2, kpi))):
                                for s8 in range(TAU):
                                    e = (TAU - 1 - s8) if d == 0 else s8
                                    c_a = off // TAU + k0
                                    P.op(T, "matmul", pp[:, 0:Wk], wbr[:, r, e, :], UT[:, s8, c_a:c_a + Wk],
                                         start=(s8 == 0), stop=(s8 == TAU - 1), reads=[kwb, "UT"], writes=[kk])
                            br = pr_[:, 0:Wk] if d == 0 else pr_[:, Wk - 1::-1] if True else None
                            bi = pi2[:, 0:Wk] if d == 0 else pi2[:, Wk - 1::-1]
                            K = lambda n_: (n_, b)
                            A_ = lambda n_: wk[n_][b][:, 0:Wk]
                            TR = Tr[d][:, 0:Wk]; TI = Ti[d][:, 0:Wk]
                            P.op(V, "tensor_tensor", A_("t1"), br, TR, ALU.mult, reads=[kpr, tk], writes=[K("t1")])
                            P.op(V, "tensor_tensor", A_("t2"), bi, TI, ALU.mult, reads=[kpi, tk], writes=[K("t2")])
                            P.op(V, "tensor_tensor", A_("t3"), bi, TR, ALU.mult, reads=[kpi, tk], writes=[K("t3")])
                            P.op(V, "tensor_tensor", A_("t4"), br, TI, ALU.mult, reads=[kpr, tk], writes=[K("t4")])
                            P.op(G_, "tensor_tensor", A_("Mr"), A_("t1"), A_("t2"), ALU.add, reads=[K("t1"), K("t2")], writes=[K("Mr")])
                            P.op(G_, "tensor_tensor", A_("Mi"), A_("t3"), A_("t4"), ALU.subtract, reads=[K("t3"), K("t4")], writes=[K("Mi")])
                            mg = MAG8[:, dG:dG + 1]
                            if wi == 0:
                                ir, ii = 0.0, 0.0
                                rk = []
                            else:
                                ir, ii = cry[:, 0:1], cry[:, 1:2]
                                rk = ["cry"]
                            P.op(V, "tensor_tensor_scan", A_("Gr"), mg.to_broadcast([128, Wk]), A_("Mr"), ir, ALU.mult, ALU.add,
                                 reads=[K("Mr"), "mag8"] + rk, writes=[K("Gr")])
                            P.op(V, "tensor_tensor_scan", A_("Gi"), mg.to_broadcast([128, Wk]), A_("Mi"), ii, ALU.mult, ALU.add,
                                 reads=[K("Mi"), "mag8"] + rk, writes=[K("Gi")])
                            c0 = kbase[si] + k0 + (1 if d == 0 else 0)
                            if d == 0:
                                ho_r = HB[d][0][:, c0:c0 + Wk]; ho_i = HB[d][1][:, c0:c0 + Wk]
                                lastc = c0 + Wk - 1
                            else:
                                ho_r = HB[d][0][:, c0 + Wk - 1:(c0 - 1 if c0 > 0 else None):-1]
                                ho_i = HB[d][1][:, c0 + Wk - 1:(c0 - 1 if c0 > 0 else None):-1]
                                lastc = c0
                            kh = ("HB", d, si, w_)
                            P.op(V, "tensor_tensor", A_("t1"), A_("Gr"), TR, ALU.mult, reads=[K("Gr"), tk], writes=[K("t1")])
                            P.op(V, "tensor_tensor", A_("t2"), A_("Gi"), TI, ALU.mult, reads=[K("Gi"), tk], writes=[K("t2")])
                            P.op(G_, "tensor_tensor", A_("t3"), A_("Gr"), TI, ALU.mult, reads=[K("Gr"), tk], writes=[K("t3")])
                            P.op(G_, "tensor_tensor", A_("t4"), A_("Gi"), TR, ALU.mult, reads=[K("Gi"), tk], writes=[K("t4")])
                            P.op(G_, "tensor_tensor", ho_r, A_("t1"), A_("t2"), ALU.subtract, reads=[K("t1"), K("t2")], writes=[kh])
                            P.op(V, "scalar_tensor_tensor", ho_i, A_("t3"), -1.0, A_("t4"), ALU.mult, ALU.subtract,
                                 reads=[K("t3"), K("t4"), kh], writes=[kh])
                            if wi < nw - 1:
                                ur = PWR(3)[:, dG:dG + 1]; ui = PWI(3)[:, dG:dG + 1]; nur = nU8r[:, dG:dG + 1]
                                hl = f32(HB[d][0][:, lastc:lastc + 1]); nhl = f32(HB[d][1][:, lastc:lastc + 1])
                                P.op(V, "tensor_scalar_mul", cry[:, 2:3], hl, ur, reads=[kh], writes=["cry2"])
                                P.op(V, "tensor_scalar_mul", cry[:, 3:4], hl, ui, reads=[kh], writes=["cry3"])
                                P.op(V, "scalar_tensor_tensor", cry[:, 0:1], nhl, ui, cry[:, 2:3], ALU.mult, ALU.add,
                                     reads=[kh, "cry2", "cry"], writes=["cry"])
                                P.op(V, "scalar_tensor_tensor", cry[:, 1:2], nhl, nur, cry[:, 3:4], ALU.mult, ALU.add,
                                     reads=[kh, "cry3", "nU8r", "cry"], writes=["cry"])
                for si, (off, L) in enumerate(seqs):
                    Kc = L // TAU
                    NKB = min(256, Kc)
                    nwin = max(1, Kc // min(W, Kc))
                    hk = [("HB", d, si, w) for d in range(2) for w in range(nwin)] + [("HBz", d, r, si) for d in range(2) for r in range(2)]
                    for kb0 in range(0, Kc, NKB):
                        kc0 = kbase[si] + kb0
                        ku0 = off // TAU + kb0
                        po = psO; kpo = "psO"
                        for t in range(TAU):
                            ov = po[:, t * NKB:(t + 1) * NKB]
                            mm = []
                            mm.append((VC[:, 0, t + 1, 0, :], HB[0][0][:, kc0:kc0 + NKB], [("VC", 0)] + hk))
                            mm.append((VC[:, 0, t + 1, 1, :], HB[0][1][:, kc0:kc0 + NKB], [("VC", 0)] + hk))
                            mm.append((VC[:, 1, TAU - t, 0, :], HB[1][0][:, kc0 + 1:kc0 + 1 + NKB], [("VC", 1)] + hk))
                            mm.append((VC[:, 1, TAU - t, 1, :], HB[1][1][:, kc0 + 1:kc0 + 1 + NKB], [("VC", 1)] + hk))
                            for s8 in range(TAU):
                                uv = UT[:, s8, ku0:ku0 + NKB]
                                if s8 <= t:
                                    mm.append((MI[:, 0, t - s8, :], uv, [("MI", 0), "UT"]))
                                if s8 >= t:
                                    mm.append((MI[:, 1, s8 - t, :], uv, [("MI", 1), "UT"]))
                                if s8 == t and G4 == 0:
                                    mm.append((dpd[:, :], uv, ["dpd", "UT"]))
                            if "S5_nostate" in _SKIP:
                                mm = mm[4:]
                            if "S5_nointra" in _SKIP:
                                mm = mm[:4]
                            for mi_, (lh, rh, rk_) in enumerate(mm):
                                P.op(T, "matmul", ov, lh, rh, start=(mi_ == 0), stop=(mi_ == len(mm) - 1), reads=rk_, writes=[kpo])
                        t0 = off + TAU * kb0
                        ya = yacc[:, t0:t0 + TAU * NKB].rearrange("q (k t) -> q k t", t=TAU)
                        pov = po[:, 0:TAU * NKB].rearrange("q (t k) -> q k t", t=TAU)
                        ky = ("yacc", t0)
                        if G4 == 0:
                            P.op(V, "tensor_copy", ya, pov, reads=[kpo], writes=[ky])
                        else:
                            P.op(V, "tensor_tensor", ya, ya, pov, ALU.add, reads=[kpo, ky], writes=[ky])
            P.dma(YS[ct * 128:(ct + 1) * 128, :], yacc[:, :], reads=[("yacc", t) for t in range(0, Ltot, TAU)], writes=[("YS", ct)])
        if debug:
            dbg_sp = nc.dram_tensor("dbg_sp", [128, 1600], F32, kind="ExternalOutput").ap()
            dbg_hb = nc.dram_tensor("dbg_hb", [4, 128, kb_], F32, kind="ExternalOutput").ap()
            P.dma(dbg_sp[:, :], sp_[:, :], reads=[])
            hbt = P.sb("hbt", [128, kb_])
            for d in range(2):
                for r in range(2):
                    allk = [("HB", d, si, w) for si in range(2) for w in range(4)] + [("HBz", d, r, si) for si in range(2)]
                    P.op(V, "tensor_copy", hbt[:], f32(HB[d][r][:, :]), reads=allk, writes=["hbt"])
                    P.dma(dbg_hb[d * 2 + r], hbt[:], reads=["hbt"], writes=[("dbg_hb", d, r)])

    if 'B2' not in _SKIP:
      with P.phase():
        hbias = P.sb("hbias", [128, 1024])
        P.dma(hbias[:], I["hbias"][:, :], writes=["hbias"])
        psH = [P.ps("psH%d" % i, [128, 512]) for i in range(8)]
        pcnt = [0]
        def slot():
            i = pcnt[0] % 8; pcnt[0] += 1
            return psH[i], ("psH", i)
        KG = 4
        for si, (off, L) in enumerate(seqs):
          with P.phase():
              N, N1, N2 = fft_dims(L)
              Hh = N1 // 2
              CB = 256 // N1
              cn = lambda n, si=si: "c%d_%s" % (si, n)
              C_ = {}
              for n, shp in (("F1", [N1, 2 * N1]), ("F2re", [N2, N2]), ("F2im", [N2, N2]), ("nF2im", [N2, N2]),
                             ("G2a", [N2, 2 * N2]), ("G2b", [N2, 2 * N2]), ("G1re", [N1, Hh]), ("nG1im", [N1, Hh]),
                             ("TWrr", [N2, 2 * N1]), ("TWis", [N2, 2 * N1]), ("TcRR", [N1, 2 * N2]), ("TcIS", [N1, 2 * N2])):
                  if n.startswith("T"):
                      C_[n] = P.sb("hc%d_%s" % (si, n), shp)
                      P.dma(C_[n][:], I[cn(n)][:, :], writes=[cn(n)])
                  else:
                      stg_ = P.sb("hcs%d_%s" % (si, n), shp)
                      P.dma(stg_[:], I[cn(n)][:, :], writes=[cn(n) + "f"])
                      C_[n] = P.sb("hcr%d_%s" % (si, n), shp, R32)
                      rcopy(S, C_[n][:], stg_[:], [cn(n) + "f"], [cn(n)])
              def tiles(n, shp, dt=F32):
                  return [P.sb("h%d_%s%d" % (si, n, i), shp, dt) for i in range(KG)]
              kf = tiles("kf", [N1, CB, N2]); vv = tiles("v", [Hh, CB, N2]); gg = tiles("g", [Hh, CB, N2])
              kfr = tiles("kfr", [N1, CB, N2], R32); dr = tiles("dr", [Hh, CB, N2], R32)
              Atk = tiles("Atk", [N2, CB, 2, N1], R32); Atx = tiles("Atx", [N2, CB, 2, N1], R32)
              TW_ = 2 * CB * max(N1, N2)
              t1k = tiles("t1k", [128, TW_]); t2k = tiles("t2k", [128, TW_]); t1x = tiles("t1x", [128, TW_]); t2x = tiles("t2x", [128, TW_])
              Ks = tiles("Ks", [N2, 2, CB, N1]); Ys = tiles("Ys", [N2, 2, CB, N1], R32); Zt = tiles("Zt", [N1, CB, 2, N2], R32)
              zz = tiles("z", [Hh, CB, N2]); tb = tiles("tb", [Hh, CB, N2])
              twr = C_["TWrr"][:, :].rearrange("p (h f) -> p h f", h=2).unsqueeze(1).to_broadcast([N2, CB, 2, N1])
              twi = C_["TWis"][:, :].rearrange("p (h f) -> p h f", h=2).unsqueeze(1).to_broadcast([N2, CB, 2, N1])
              tcr = C_["TcRR"][:, :].rearrange("p (h f) -> p h f", h=2).unsqueeze(1).to_broadcast([N1, CB, 2, N2])
              tci = C_["TcIS"][:, :].rearrange("p (h f) -> p h f", h=2).unsqueeze(1).to_broadcast([N1, CB, 2, N2])

              def conv_group(c0, b, si=si, off=off, L=L, N1=N1, N2=N2, Hh=Hh, CB=CB):
                  tsl = slice(off, off + L)
                  def ld(dst, srcap, key):
                      P.dma(dst, srcap.rearrange("c (a b) -> a c b", b=N2), writes=[key])
                  def f1(src, srck, Kdim):
                      p1, k1 = slot()
                      p1v = p1[:N2, :].rearrange("p (c h f) -> p c h f", c=CB, h=2)
                      for c in range(CB):
                          P.op(T, "matmul", p1[:N2, c * 2 * N1:(c + 1) * 2 * N1], src[:, c, :], C_["F1"][:Kdim, :], start=True, stop=True,
                               reads=[srck, cn("F1")], writes=[k1])
                      return p1v, k1
                  def twid(p1v, k1, t1_, t2_, at, kt1, kt2, ka, ae=G_):
                      t1v = t1_[:N2, 0:CB * 2 * N1].rearrange("p (c h f) -> p c h f", c=CB, h=2)
                      t2v = t2_[:N2, 0:CB * 2 * N1].rearrange("p (c h f) -> p c h f", c=CB, h=2)
                      P.op(V, "tensor_tensor", t1v, p1v, twr, ALU.mult, reads=[k1, cn("TWrr")], writes=[kt1])
                      P.op(V, "tensor_tensor", t2v, p1v[:, :, ::-1, :], twi, ALU.mult, reads=[k1, cn("TWis")], writes=[kt2])
                      P.op(ae, "tensor_tensor", at[:], t1v, t2v, ALU.add, reads=[kt1, kt2], writes=[ka])
                  def f2(at, ka):
                      p2, k2 = slot()
                      p2v = p2[:N2, :].rearrange("p (h c f) -> p h c f", h=2, c=CB)
                      ar = at[:, :, 0, :]; ai = at[:, :, 1, :]
                      P.op(T, "matmul", p2v[:, 0, :, :], C_["F2re"][:, :], ar, start=True, stop=False, reads=[ka, cn("F2re")], writes=[k2])
                      P.op(T, "matmul", p2v[:, 0, :, :], C_["nF2im"][:, :], ai, start=False, stop=True, reads=[ka, cn("nF2im")], writes=[k2])
                      P.op(T, "matmul", p2v[:, 1, :, :], C_["F2im"][:, :], ar, start=True, stop=False, reads=[ka, cn("F2im")], writes=[k2])
                      P.op(T, "matmul", p2v[:, 1, :, :], C_["F2re"][:, :], ai, start=False, stop=True, reads=[ka, cn("F2re")], writes=[k2])
                      return p2v, k2
                  K_ = lambda n: (n, si, b)
                  ld(vv[b][:], VS[0][c0:c0 + CB, tsl], K_("vv"))
                  zprev, zpk = vv[b], K_("vv")
                  for o in range(2):
                      ld(kf[b][:], KT[si][o, c0:c0 + CB, :], K_("kf"))
                      ld(gg[b][:], VS[1 + o][c0:c0 + CB, tsl], K_("gg"))
                      rcopy(S, kfr[b][:], kf[b][:], [K_("kf")], [K_("kfr")])
                      rcopy(S, dr[b][:], zprev[:], [zpk], [K_("dr")])
                      yield
                      pk1, kk1 = f1(kfr[b], K_("kfr"), N1)
                      px1, kx1 = f1(dr[b], K_("dr"), Hh)
                      yield
                      twid(pk1, kk1, t1k[b], t2k[b], Atk[b], K_("t1k"), K_("t2k"), K_("Atk"), ae=V)
                      twid(px1, kx1, t1x[b], t2x[b], Atx[b], K_("t1x"), K_("t2x"), K_("Atx"))
                      yield
                      pk2, kk2 = f2(Atk[b], K_("Atk"))
                      px, kx = f2(Atx[b], K_("Atx"))
                      yield
                      P.op(S, "activation", Ks[b][:], pk2, AF.Copy, reads=[kk2], writes=[K_("Ks")])
                      t1v = t1k[b][:N2, 0:2 * CB * N1].rearrange("p (h c f) -> p h c f", h=2, c=CB)
                      t2v = t2k[b][:N2, 0:2 * CB * N1].rearrange("p (h c f) -> p h c f", h=2, c=CB)
                      kr = Ks[b][:, 0:1, :, :].to_broadcast([N2, 2, CB, N1])
                      ki = Ks[b][:, 1:2, :, :].to_broadcast([N2, 2, CB, N1])
                      P.op(V, "tensor_tensor", t1v, px, kr, ALU.mult, reads=[kx, K_("Ks")], writes=[K_("t1k")])
                      P.op(V, "tensor_tensor", t2v, px[:, ::-1, :, :], ki, ALU.mult, reads=[kx, K_("Ks")], writes=[K_("t2k")])
                      P.op(G_, "tensor_tensor", Ys[b][:, 0, :, :], t1v[:, 0, :, :], t2v[:, 0, :, :], ALU.subtract,
                           reads=[K_("t1k"), K_("t2k")], writes=[K_("Ys")])
                      P.op(G_, "tensor_tensor", Ys[b][:, 1, :, :], t1v[:, 1, :, :], t2v[:, 1, :, :], ALU.add,
                           reads=[K_("t1k"), K_("t2k"), K_("Ys")], writes=[K_("Ys")])
                      yield
                      p3, k3 = slot()
                      p3v = p3[:N1, :].rearrange("p (c h f) -> p c h f", c=CB, h=2)
                      for c in range(CB):
                          P.op(T, "matmul", p3[:N1, c * 2 * N2:(c + 1) * 2 * N2], Ys[b][:, 0, c, :], C_["G2a"][:, :], start=True, stop=False,
                               reads=[K_("Ys"), cn("G2a")], writes=[k3])
                          P.op(T, "matmul", p3[:N1, c * 2 * N2:(c + 1) * 2 * N2], Ys[b][:, 1, c, :], C_["G2b"][:, :], start=False, stop=True,
                               reads=[K_("Ys"), cn("G2b")], writes=[k3])
                      yield
                      u1 = t1x[b][:N1, 0:CB * 2 * N2].rearrange("p (c h f) -> p c h f", c=CB, h=2)
                      u2 = t2x[b][:N1, 0:CB * 2 * N2].rearrange("p (c h f) -> p c h f", c=CB, h=2)
                      P.op(V, "tensor_tensor", u1, p3v, tcr, ALU.mult, reads=[k3, cn("TcRR")], writes=[K_("t1x")])
                      P.op(V, "tensor_tensor", u2, p3v[:, :, ::-1, :], tci, ALU.mult, reads=[k3, cn("TcIS")], writes=[K_("t2x")])
                      P.op(V if (c0 // CB) % 2 else G_, "tensor_tensor", Zt[b][:], u1, u2, ALU.add, reads=[K_("t1x"), K_("t2x")], writes=[K_("Zt")])
                      yield
                      p4, k4 = slot()
                      p4v = p4[:Hh, 0:CB * N2].rearrange("p (c f) -> p c f", c=CB)
                      P.op(T, "matmul", p4v, C_["G1re"][:, :], Zt[b][:, :, 0, :], start=True, stop=False,
                           reads=[K_("Zt"), cn("G1re")], writes=[k4])
                      P.op(T, "matmul", p4v, C_["nG1im"][:, :], Zt[b][:, :, 1, :], start=False, stop=True,
                           reads=[K_("Zt"), cn("nG1im")], writes=[k4])
                      yield
                      bia = hbias[:Hh, o * 512 + c0:o * 512 + c0 + CB].unsqueeze(2).to_broadcast([Hh, CB, N2])
                      P.op(G_, "tensor_tensor", tb[b][:], zprev[:], bia, ALU.mult, reads=[zpk, "hbias"], writes=[K_("tb")])
                      P.op(V, "tensor_tensor", tb[b][:], tb[b][:], p4v, ALU.add, reads=[K_("tb"), k4], writes=[K_("tb")])
                      P.op(G_, "tensor_tensor", zz[b][:], tb[b][:], gg[b][:], ALU.mult, reads=[K_("tb"), K_("gg"), zpk], writes=[K_("zz")])
                      if o == 0:
                          P.op(S, "activation", vv[b][:], zz[b][:], AF.Copy, reads=[K_("zz")], writes=[K_("vv")])
                          zprev, zpk = vv[b], K_("vv")
                      yield
                  P.dma(YH[c0:c0 + CB, tsl].rearrange("c (a b) -> a c b", b=N2), zz[b][:], reads=[K_("zz")], writes=[("YH", si, c0)])

              pending = list(range(0, DH, CB))
              active = []
              free_b = list(range(KG))
              while pending or active:
                  while pending and free_b:
                      bb_ = free_b.pop(0)
                      active.append((conv_group(pending.pop(0), bb_), bb_))
                  for ent in list(active):
                      try:
                          next(ent[0])
                      except StopIteration:
                          active.remove(ent)
                          free_b.append(ent[1])

    if 'C' not in _SKIP:
      with P.phase():
          tl, psG, psU, psD, ps_s, ps_q = alloc_row_tiles()
          xT = tl["xT"]; hT = tl["hT"]; xTr = tl["xTr"]; stg = tl["stg"]; gm = tl["gm"]
          YSv = YS.rearrange("(k p) t -> p k t", p=128)
          YHv = YH.rearrange("(k p) t -> p k t", p=128)
          gluv = I["gluw"].rearrange("(kc p) (n c) -> n p kc c", p=128, c=128)
          woutv = I["wout"].rearrange("(kc p) (n c) -> n p kc c", p=128, c=128)
          rs = [P.sb("rs%d" % i, [128, TT]) for i in range(2)]
          wseq = []
          for ti in range(Ltot // TT):
              wseq += [("gluw", o, 4, "b") for o in range(4)] + [("wout", dch, KC, "b") for dch in range(KC)] + ffn_seq("wg2", "wu2", "wd2")
          ws_ = WStream(tl, wseq)
          yin = P.sb("yin", [128, 8, TT]); x1in = P.sb("x1in", [128, 8, TT])
          def load_c(tj):
              P.dma(yin[:, 0:4, :], YSv[:, :, tj * TT:(tj + 1) * TT], writes=[("yin", f) for f in range(0, 4)])
              P.dma(yin[:, 4:8, :], YHv[:, :, tj * TT:(tj + 1) * TT], writes=[("yin", f) for f in range(4, 8)])
          def load_x1(tj):
              P.dma(x1in[:, :, :], X1Tv[:, :, tj * TT:(tj + 1) * TT], writes=[("x1in", k) for k in range(KC)])
          load_c(0); load_x1(0)
          for ti in range(Ltot // TT):
              t0 = ti * TT
              for k in range(4):
                  ys = yin[:, k, :]; g = gm[:, k, :]; tm = stg[:, 8 + (k % 2), :]
                  kys, kg, ktm = ("yin", k), ("gm", k), ("stg", 8 + (k % 2))
                  P.op(S, "activation", tm, ys, AF.Square, reads=[kys], writes=[ktm])
                  P.op(V, "tensor_scalar", tm, tm, 0.044715, 1.0, ALU.mult, ALU.add, reads=[ktm], writes=[ktm])
                  P.op(V, "tensor_tensor", tm, tm, ys, ALU.mult, reads=[ktm, kys], writes=[ktm])
                  P.op(S, "activation", tm, tm, AF.Sigmoid, scale=1.5957691216057308, reads=[ktm], writes=[ktm])
                  P.op(G_, "tensor_tensor", ys, ys, tm, ALU.mult, reads=[kys, ktm], writes=[kys])
                  rcopy(S, g, ys, [kys], [kg])
              for o in range(4):
                  b = o % 2
                  wt_, kw_ = ws_.get()
                  for k in range(4):
                      P.op(T, "matmul", psD[b][:, :], wt_[:, k, :], gm[:, k, :], start=(k == 0), stop=(k == 3),
                           reads=[kw_, ("gm", k)], writes=[("psD", b)])
                  P.op(S, "activation", tl["s"][b][:], psD[b][:, :], AF.Sigmoid, bias=col("glub", o),
                       reads=[("psD", b), "cols"], writes=[("s", b)])
                  P.op(V, "tensor_tensor", yin[:, o, :], yin[:, o, :], tl["s"][b][:], ALU.mult,
                       reads=[("yin", o), ("s", b)], writes=[("yin", o)])
              for hgi, (base, gn, pst, kps) in enumerate(((0, "sng", ps_s, "ps_s"), (4, "hng", ps_q, "ps_q"))):
                  for k in range(4):
                      sq = tl["sq"][k % 2]
                      P.op(S, "activation", sq[:], yin[:, base + k, :], AF.Square, reads=[("yin", base + k)], writes=[("sq", k % 2)])
                      P.op(T, "matmul", pst[:, :], ones[:, :], sq[:], start=(k == 0), stop=(k == 3),
                           reads=["ones", ("sq", k % 2)], writes=[kps])
                  P.op(S, "activation", rs[hgi][:], pst[:, :], AF.Sqrt, bias=cst[:, 1:2], scale=1.0 / 512,
                       reads=[kps, "cst1"], writes=[("rs", hgi)])
                  P.op(V, "reciprocal", rs[hgi][:], rs[hgi][:], reads=[("rs", hgi)], writes=[("rs", hgi)])
                  for k in range(4):
                      P.op(V, "scalar_tensor_tensor", gm[:, 4 + base + k, :], yin[:, base + k, :], col(gn, k), rs[hgi][:],
                           ALU.mult, ALU.mult, reads=[("yin", base + k), ("rs", hgi), "cols"], writes=[("gm", 4 + base + k)])
              if ti + 1 < Ltot // TT:
                  load_c(ti + 1)
              for dch in range(KC):
                  b = dch % 2
                  wt_, kw_ = ws_.get()
                  for k in range(KC):
                      P.op(T, "matmul", psD[b][:, :], wt_[:, k, :], gm[:, 4 + k, :], start=(k == 0), stop=(k == KC - 1),
                           reads=[kw_, ("gm", 4 + k)], writes=[("psD", b)])
                  P.op(V, "scalar_tensor_tensor", xT[:, dch, :], x1in[:, dch, :], ALPHA, psD[b][:, :], ALU.mult, ALU.add,
                       reads=[("x1in", dch), ("psD", b)], writes=[("xT", dch)])
              if ti + 1 < Ltot // TT:
                  load_x1(ti + 1)
              ln_inplace(xT, "xT", "ln2g", "ln2b", tl, ps_s, ps_q, want_r=False, want_b=True)
              ffn_inplace(xT, "xT", ws_, tl, psG, psU, psD)
              ln_inplace(xT, "xT", "ln3g", "ln3b", tl, ps_s, ps_q, want_r=False)
              for sub in range(4):
                  for half in range(2):
                      b = half
                      for k4 in range(4):
                          k = half * 4 + k4
                          P.op(T, "transpose", psD[b][:, k4 * 128:(k4 + 1) * 128], xT[:, k, sub * 128:(sub + 1) * 128], ident[:, :],
                               reads=[("xT", k), "ident"], writes=[("psD", b)])
                      rcopy(S if half else V, stg[:, 2 * sub + half, :], psD[b][:, :], [("psD", b)], [("stg", 2 * sub + half)])
                  P.dma(yout[t0 + sub * 128:t0 + (sub + 1) * 128, :].rearrange("t (a b) -> t a b", b=TT), stg[:, 2 * sub:2 * sub + 2, :],
                        reads=[("stg", 2 * sub), ("stg", 2 * sub + 1)], writes=[("yout", ti, sub)])

    P.emit()
    return nc, P


_CACHE = {}


def run(inputs, Lp, Ls, n_cores=8, debug=False):
    in_maps = [host_inputs(inputs, c, Lp, Ls) for c in range(n_cores)]
    shapes = {k: v.shape for k, v in in_maps[0].items()}
    nc, P = build(Lp, Ls, shapes, debug=debug)
    res = run_bass_kernel_spmd(nc, in_maps, core_ids=list(range(n_cores)))
    return res, P


def kernel(**inputs):
    Lp = inputs["x_prompt"].shape[1]
    Ls = inputs["x_sample"].shape[1]
    res, _ = run(inputs, Lp, Ls)
    yp = np.stack([res.results[c]["y"][:Lp] for c in range(8)], 0).astype(np.float32)
    ys = np.stack([res.results[c]["y"][Lp:] for c in range(8)], 0).astype(np.float32)
    return (yp, ys)
```
